# Optimizing a Trainium2 kernel written in Bass

```python
import jax, jax.numpy as jnp
from jax import lax
import numpy as np

D_MODEL = 1024
BATCH = 32
SEQ = 2048
DEPTH = 1

CHUNK = 64
EXPAND = 2
E_TOTAL = EXPAND * D_MODEL
E_POOL = E_TOTAL // 2
E_CONV = E_TOTAL // 2
POOL_WINDOWS = (2, 4, 8, 16)
N_POOL_GROUPS = len(POOL_WINDOWS)
POOL_GROUP = E_POOL // N_POOL_GROUPS
CONV_K = 3
N_BRANCHES = 2
RMS_EPS = 1e-6

IN_SPLITS = (E_POOL, E_POOL,
             E_CONV, E_CONV, E_CONV, E_CONV,
             D_MODEL, D_MODEL)
IN_WIDTH = sum(IN_SPLITS)

kernel_name = "hybrid_pool_shortconv_gated_merge_adaln"


def rmsnorm(x, g):
    xf = x.astype(jnp.float32)
    r = lax.rsqrt(jnp.mean(xf * xf, axis=-1, keepdims=True) + RMS_EPS)
    return (xf * r).astype(x.dtype) * g


def split_cols(z):
    idx = np.cumsum(IN_SPLITS)[:-1].tolist()
    return jnp.split(z, idx, axis=-1)


def multiscale_pool_residual(u, pool_w):
    b, s, _ = u.shape
    uf = u.astype(jnp.float32)
    cs = jnp.pad(jnp.cumsum(uf, axis=1), ((0, 0), (1, 0), (0, 0)))
    pos = jnp.arange(1, s + 1, dtype=jnp.float32)
    groups = []
    for gi, w in enumerate(POOL_WINDOWS):
        sl = slice(gi * POOL_GROUP, (gi + 1) * POOL_GROUP)
        c_g = cs[:, :, sl]
        lagged = jnp.pad(c_g, ((0, 0), (w, 0), (0, 0)))[:, : s + 1]
        wsum = c_g[:, 1:] - lagged[:, 1:]
        cnt = jnp.minimum(pos, float(w))[None, :, None]
        groups.append(wsum / cnt - uf[:, :, sl])
    pooled = jnp.stack(groups, axis=2)
    mixed = jnp.einsum("bsgi,gio->bsgo", pooled, pool_w.astype(jnp.float32))
    return mixed.reshape(b, s, E_POOL).astype(u.dtype)


def causal_depthwise_conv(v, k, bias):
    s = v.shape[1]
    vp = jnp.pad(v, ((0, 0), (CONV_K - 1, 0), (0, 0)))
    out = bias
    for j in range(CONV_K):
        out = out + vp[:, j:j + s] * k[j]
    return out


def setup_inputs(seed: int = 0) -> dict:
    key = jax.random.key(seed)
    ks = jax.random.split(key, 16)
    f32 = jnp.float32
    nrm = lambda k, shape, sc: jax.random.normal(k, shape, f32) * sc
    return {
        "x": nrm(ks[0], (BATCH, SEQ, D_MODEL), 1.0),
        "c": nrm(ks[1], (BATCH, D_MODEL), 1.0),
        "ada_w": nrm(ks[2], (DEPTH, D_MODEL, 3 * D_MODEL), 0.2 * D_MODEL ** -0.5),
        "ada_b": nrm(ks[3], (DEPTH, 3 * D_MODEL), 0.02),
        "norm_g": 1.0 + nrm(ks[4], (DEPTH, D_MODEL), 0.05),
        "w_in": nrm(ks[5], (DEPTH, D_MODEL, IN_WIDTH), D_MODEL ** -0.5),
        "b_in": nrm(ks[6], (DEPTH, IN_WIDTH), 0.02),
        "pool_w": nrm(ks[7], (DEPTH, N_POOL_GROUPS, POOL_GROUP, POOL_GROUP), POOL_GROUP ** -0.5),
        "pool_scale": 1.0 + nrm(ks[8], (DEPTH, E_POOL), 0.1),
        "conv_w": nrm(ks[9], (DEPTH, CONV_K, E_CONV), CONV_K ** -0.5),
        "conv_b": nrm(ks[10], (DEPTH, E_CONV), 0.02),
        "w_out_a": nrm(ks[11], (DEPTH, E_POOL, D_MODEL), E_POOL ** -0.5),
        "w_out_b": nrm(ks[12], (DEPTH, E_CONV, D_MODEL), E_CONV ** -0.5),
        "w_o": nrm(ks[13], (DEPTH, D_MODEL, D_MODEL), D_MODEL ** -0.5),
        "final_g": 1.0 + nrm(ks[14], (D_MODEL,), 0.05),
    }


def reference(x, c, ada_w, ada_b, norm_g, w_in, b_in, pool_w, pool_scale,
              conv_w, conv_b, w_out_a, w_out_b, w_o, final_g):
    c_act = jax.nn.silu(c)
    for l in range(DEPTH):
        mod = c_act @ ada_w[l] + ada_b[l]
        shift, scale, gate = jnp.split(mod, 3, axis=-1)
        h = rmsnorm(x, norm_g[l]) * (1.0 + scale[:, None, :]) + shift[:, None, :]

        z = h @ w_in[l] + b_in[l]
        a_v, a_g, b_B, b_C, b_v, b_g, m_a, m_b = split_cols(z)

        y_a = multiscale_pool_residual(a_v, pool_w[l]) * pool_scale[l] * jax.nn.silu(a_g)
        o_a = y_a @ w_out_a[l]

        y_b = b_B * causal_depthwise_conv(b_C * b_v, conv_w[l], conv_b[l]) * jax.nn.silu(b_g)
        o_b = y_b @ w_out_b[l]

        merged = jax.nn.sigmoid(m_a) * o_a + jax.nn.sigmoid(m_b) * o_b
        x = x + gate[:, None, :] * (merged @ w_o[l])
    return rmsnorm(x, final_g)
```

```python
import numpy as np
from contextlib import ExitStack
import concourse.bass as bass
import concourse.mybir as mybir
from concourse.bass_utils import run_bass_kernel_spmd

F32 = mybir.dt.float32
BF16 = mybir.dt.bfloat16
I32 = mybir.dt.int32
AF = mybir.ActivationFunctionType
ALU = mybir.AluOpType

D = 1024
SEQ = 2048
NCORES = 8
TP = 1024
NCH = 80
NSLOT = 5
EPS = 1e-6
WINDOWS = (2, 4, 8, 16)


class Buf:
    __slots__ = ("name", "w", "r")

    def __init__(self, name):
        self.name = name
        self.w = None
        self.r = {}


class Prog:
    ENGS = ("pe", "act", "dve", "pool", "sp")

    def __init__(self, nc, stack):
        self.nc = nc
        self.stack = stack
        self.q = {e: [] for e in self.ENGS}
        self.cnt = {e: 0 for e in self.ENGS}
        self.known = {e: {} for e in self.ENGS}
        self.esem = {}
        for e in ("pe", "act", "dve", "pool"):
            self.esem[e] = stack.enter_context(nc.semaphore("s_" + e))
        self.dsem = {}
        self.dcnt = {}

    def _waits(self, eng, reads, writes):
        need = {}

        def add(tok):
            if tok is None:
                return
            s, v = tok
            if eng == "pe" and s == ("e", "pe"):
                return
            if need.get(s, 0) < v:
                need[s] = v

        for b in reads:
            add(b.w)
        for b in writes:
            add(b.w)
            for t in b.r.items():
                add(t)
        out = []
        kn = self.known[eng]
        for s, v in need.items():
            if kn.get(s, 0) >= v:
                continue
            kn[s] = v
            out.append((s, v))
        return out

    def _mark(self, tok, reads, writes):
        for b in reads:
            if b.r.get(tok[0], 0) < tok[1]:
                b.r[tok[0]] = tok[1]
        for b in writes:
            b.w = tok
            b.r = {}

    def op(self, eng, fn, reads=(), writes=()):
        waits = self._waits(eng, reads, writes)
        self.cnt[eng] += 1
        tok = (("e", eng), self.cnt[eng])
        self.q[eng].append((fn, waits, ("e", eng), 1))
        self._mark(tok, reads, writes)
        return tok

    def dma(self, eng, semname, fn, reads=(), writes=()):
        if semname not in self.dsem:
            self.dsem[semname] = self.stack.enter_context(self.nc.semaphore("d_" + semname))
            self.dcnt[semname] = 0
        waits = self._waits(eng, reads, writes)
        prev = self.dcnt[semname]
        s = ("d", semname)
        if prev > 0 and self.known[eng].get(s, 0) < prev:
            self.known[eng][s] = prev
            waits.append((s, prev))
        self.dcnt[semname] += 16
        tok = (s, self.dcnt[semname])
        self.q[eng].append((fn, waits, s, 16))
        self._mark(tok, reads, writes)
        return tok

    def final_wait(self, eng):
        waits = [(("d", n), v) for n, v in self.dcnt.items() if v > 0]
        self.q[eng].append((None, waits, None, 0))

    def _sem(self, s):
        kind, name = s
        return self.esem[name] if kind == "e" else self.dsem[name]

    def emit(self):
        nc = self.nc
        with nc.Block() as block:
            def run(engname):
                def body(e):
                    for fn, waits, s, inc in self.q[engname]:
                        for (ws, wv) in waits:
                            e.wait_ge(self._sem(ws), wv)
                        if fn is None:
                            continue
                        ins = fn(e)
                        ins.then_inc(self._sem(s), inc)
                return body

            block.tensor(run("pe"))
            block.scalar(run("act"))
            block.vector(run("dve"))
            block.gpsimd(run("pool"))
            block.sync(run("sp"))


def build_nc(nseq):
    NT = nseq * SEQ
    NPASS = NT // TP
    nc = bass.Bass("TRN2", target_bir_lowering=False)
    dt_in = lambda n, sh: nc.dram_tensor(n, sh, F32, kind="ExternalInput").ap()
    x_d = dt_in("x", [NT, D])
    cT_d = dt_in("cT", [128, 8 * nseq])
    ws_d = dt_in("ws", [NCH, 128, 1024])
    adaw_d = dt_in("adaw", [3, 128, 8192])
    wo_d = dt_in("wo", [128, 8192])
    pw_d = dt_in("pw", [128, 2048])
    cv_d = dt_in("cvec", [128, 136])
    bc_d = dt_in("bcv", [2, 1024])
    y_d = nc.dram_tensor("y", [NT, D], F32, kind="ExternalOutput").ap()

    with ExitStack() as st:
        P = Prog(nc, st)
        sbt = lambda n, sh, dt: st.enter_context(nc.sbuf_tensor("sb_" + n, sh, dt))
        pst = lambda n, sh, dt: st.enter_context(nc.psum_tensor("ps_" + n, sh, dt))

        hT = [sbt("hT%d" % i, [128, 8, TP], BF16) for i in range(2)]
        ya = sbt("ya", [128, 8, TP], BF16)
        yb = sbt("yb", [128, 8, TP], BF16)
        mg = sbt("mg", [128, 8, TP], BF16)
        wsl = [sbt("wsl%d" % i, [128, 8, 128], BF16) for i in range(NSLOT)]
        pw = sbt("pw", [128, 4, 2, 256], BF16)
        wo = sbt("wo", [128, 8, 1024], BF16)
        gbc = sbt("gbc", [128, nseq, 1024], BF16)
        fgbc = sbt("fgbc", [128, 1024], F32)
        cvec = sbt("cvec", [128, 136], F32)
        ident = sbt("ident", [128, 128], BF16)
        cT = sbt("cT", [128, 8 * nseq], F32)
        scb = sbt("scb", [128, 8, nseq], BF16)
        mod = sbt("mod", [128, 24, nseq], F32)
        Asc = sbt("Asc", [128, nseq, 8], F32)
        gbt = sbt("gbt", [128, 2, 128], BF16)
        invc = sbt("invc", [128, 16], F32)
        mhalf = sbt("mhalf", [128, 1], F32)
        invi = sbt("invi", [128, 16], I32)
        XT = [sbt("xt%d" % i, [128, 1024], F32) for i in range(2)]
        XNB = [sbt("xnb%d" % i, [128, 1024], BF16) for i in range(2)]
        junk = sbt("junk", [128, 1024], BF16)
        NXN = 4
        XN = [sbt("xn%d" % i, [128, 1024], F32) for i in range(NXN)]
        nwt = sbt("nwt", [128, 32], F32)
        T1 = [sbt("t1_%d" % i, [128, TP], F32) for i in range(2)]
        T2 = [sbt("t2_%d" % i, [128, TP], F32) for i in range(2)]
        T3 = [sbt("t3_%d" % i, [128, TP], F32) for i in range(2)]
        V = [sbt("v_%d" % i, [128, TP + 2], F32) for i in range(2)]
        U = [sbt("u_%d" % i, [128, TP + 16], F32) for i in range(2)]
        SA = sbt("sa", [128, TP + 16], F32)
        SB = sbt("sb", [128, TP + 16], F32)
        R = [sbt("r_%d" % i, [128, TP], BF16) for i in range(2)]
        Vh = sbt("vh", [128, 8, 2], F32)
        Uh = sbt("uh", [128, 8, 16], F32)
        tfix = sbt("tfix", [128, 16], F32)

        NZ = 4
        Z = [pst("z%d" % i, [128, 1024], F32) for i in range(NZ)]
        PS = Z[0][:, 0:24 * nseq].rearrange("p (m b) -> p m b", b=nseq)

        bZ = [Buf("z%d" % i) for i in range(4)]
        bPS = bZ[0]
        bhT = [[Buf("hT%d_%d" % (i, t)) for t in range(8)] for i in range(2)]
        bya = [Buf("ya%d" % k) for k in range(8)]
        byb = [Buf("yb%d" % k) for k in range(8)]
        bmg = [Buf("mg%d" % k) for k in range(8)]
        bws = [Buf("ws%d" % i) for i in range(NSLOT)]
        bconst = Buf("const")
        bcT = Buf("cT")
        bscb = Buf("scb")
        bfg = Buf("fg")
        bpw = Buf("pw")
        bcv = Buf("cvec")
        bwo = [Buf("wo%d" % k) for k in range(8)]
        bmodg = Buf("modg")
        bmod = Buf("mod")
        bstart = Buf("start")
        bgbc = Buf("gbc")
        bXT = [Buf("xt%d" % i) for i in range(2)]
        bXNB = [Buf("xnb%d" % i) for i in range(2)]
        bjunk = Buf("junk")
        bXN = [Buf("xn%d" % i) for i in range(4)]
        bnwp = [Buf("nwp0"), Buf("nwp1")]
        bnwt = [Buf("nwt0"), Buf("nwt1")]
        bT1 = [Buf("t1_%d" % i) for i in range(2)]
        bT2 = [Buf("t2_%d" % i) for i in range(2)]
        bT3 = [Buf("t3_%d" % i) for i in range(2)]
        bV = [Buf("v%d" % i) for i in range(2)]
        bU = [Buf("u%d" % i) for i in range(2)]
        bSA = Buf("sa")
        bSB = Buf("sb")
        bR = [Buf("r%d" % i) for i in range(2)]
        bVh = [Buf("vh%d" % k) for k in range(8)]
        bUh = [Buf("uh%d" % k) for k in range(8)]
        btfix = Buf("tfix")
        bgbt = [Buf("gbt%d" % i) for i in range(2)]

        def c_adab(m):
            return cvec[:, m:m + 1]

        def c_bin(zc):
            return cvec[:, 32 + zc:33 + zc]

        def c_psc(k):
            return cvec[:, 96 + k:97 + k]

        def c_cw(k, j):
            return cvec[:, 104 + 3 * k + j:105 + 3 * k + j]

        def c_cb(k):
            return cvec[:, 128 + k:129 + k]

        order = []
        total_chunks = NPASS * NCH
        wstate = {"issued": 0}

        def issue_w(n):
            if n >= total_chunks:
                return
            slot = n % NSLOT
            src = ws_d[n % NCH]
            dst = wsl[slot]
            P.dma("pool", "w%d" % slot,
                  lambda e, dst=dst, src=src: e.dma_start(out=dst[:].rearrange("p k n -> p (k n)"), in_=src),
                  writes=[bws[slot]])

        def setup():
            P.op("pool", lambda e: e.memset(mhalf[:], -0.5), writes=[bconst])
            P.op("pool", lambda e: e.memset(ident[:], 0.0), writes=[bconst])
            P.op("pool", lambda e: e.affine_select(out=ident[:], in_=ident[:], pattern=[[-1, 128]],
                                                   compare_op=ALU.not_equal, fill=1.0, base=0, channel_multiplier=1),
                 reads=[bconst], writes=[bconst])
            P.op("pool", lambda e: e.iota(invi[:], pattern=[[1, 16]], base=1, channel_multiplier=0),
                 writes=[btfix])
            P.dma("sp", "s0", lambda e: e.dma_start(out=cvec[:], in_=cv_d), writes=[bcv])
            P.dma("sp", "s1", lambda e: e.dma_start(out=cT[:], in_=cT_d), writes=[bcT])
            P.dma("sp", "s3", lambda e: e.dma_start(out=fgbc[:], in_=bc_d[1:2, :].to_broadcast([128, 1024])),
                  writes=[bfg])
            for i in (0, 1):
                P.dma("pool", "sa%d" % i,
                      lambda e, i=i: e.dma_start(out=pieces[i][:].rearrange("p k n -> p (k n)"), in_=adaw_d[i]),
                      writes=[bpieces[i]])
            for n in range(NSLOT):
                issue_w(n)
            P.op("dve", lambda e: e.tensor_copy(out=invc[:], in_=invi[:]), reads=[btfix], writes=[bconst])
            P.op("dve", lambda e: e.reciprocal(out=invc[:], in_=invc[:]), reads=[bconst], writes=[bconst])

        def setup_b1():
            P.op("act", lambda e: e.activation(out=scb[:].rearrange("p k b -> p (k b)"), in_=cT[:], func=AF.Silu),
                 reads=[bcT], writes=[bscb])
            zb = next_z()
            psv = Z[zb][:, 0:24 * nseq].rearrange("p (m b) -> p m b", b=nseq)
            for i in (0, 1):
                P.op("pe", lambda e, i=i: ada_mm(e, i, psv), reads=[bpieces[i], bscb], writes=[bZ[zb]])
            P.op("dve", lambda e: e.tensor_tensor(out=mod[:, 0:16, :], in0=psv[:, 0:16, :],
                                                  in1=cvec[:, 0:16].unsqueeze(2).to_broadcast([128, 16, nseq]),
                                                  op=ALU.add),
                 reads=[bZ[zb], bconst, bcv, bpw, bfg], writes=[bmod])
            for b in range(nseq):
                P.op("dve", lambda e, b=b: e.scalar_tensor_tensor(out=Asc[:, b, :], in0=mod[:, 8:16, b], scalar=1.0,
                                                                  in1=cvec[:, 24:32], op0=ALU.add, op1=ALU.mult),
                     reads=[bmod, bconst, bcv, bpw, bfg], writes=[bmod])

        def setup_b2():
            P.dma("pool", "spw", lambda e: e.dma_start(out=pw[:].rearrange("p g k n -> p (g k n)"), in_=pw_d),
                  reads=[bstart], writes=[bpw])
            P.dma("pool", "sa2",
                  lambda e: e.dma_start(out=pieces[2][:].rearrange("p k n -> p (k n)"), in_=adaw_d[2]),
                  reads=[bstart], writes=[bpieces[2]])
            for fc in range(8):
                P.op("dve", lambda e, fc=fc: e.tensor_scalar(out=hT[0][:, fc, :], in0=hT[0][:, fc, :],
                                                             scalar1=Asc[:, 0, fc:fc + 1], scalar2=mod[:, fc, 0:1],
                                                             op0=ALU.mult, op1=ALU.add),
                     reads=[bmod] + bhT[0], writes=bhT[0])

        pieces = [ya, yb, mg]
        bpieces = [bya[0], byb[0], bmg[0]]

        def ada_mm(e, i, ps):
            ins = None
            for ml in range(8):
                for k in range(8):
                    ins = e.matmul(ps[:, i * 8 + ml, :], pieces[i][:, k, ml * 128:(ml + 1) * 128],
                                   scb[:, k, :], start=(k == 0), stop=(k == 7))
            return ins

        pool_work = []

        def drain_pool_work(n=1):
            for _ in range(n):
                if pool_work:
                    pool_work.pop(0)()

        def setup_gate():
            zb = next_z()
            psv = Z[zb][:, 0:24 * nseq].rearrange("p (m b) -> p m b", b=nseq)
            P.op("pe", lambda e: ada_mm(e, 2, psv), reads=[bpieces[2], bscb], writes=[bZ[zb]])
            P.op("dve", lambda e: e.tensor_tensor(out=mod[:, 16:24, :], in0=psv[:, 16:24, :],
                                                  in1=cvec[:, 16:24].unsqueeze(2).to_broadcast([128, 8, nseq]),
                                                  op=ALU.add),
                 reads=[bZ[zb], bconst, bcv, bpw, bfg], writes=[bmodg])
            for b in range(nseq):
                zb = next_z()
                for k in range(8):
                    s_ = (b * 8 + k) % 2
                    P.op("dve", lambda e, s_=s_, k=k, b=b: e.tensor_copy(
                        out=gbt[:, s_, :], in_=mod[:, 16 + k, b:b + 1].to_broadcast([128, 128])),
                        reads=[bmodg], writes=[bgbt[s_]])
                    P.op("pe", lambda e, s_=s_, k=k, zb=zb: e.matmul(Z[zb][:, k * 128:(k + 1) * 128], gbt[:, s_, :],
                                                                    ident[:], start=True, stop=True),
                         reads=[bgbt[s_], bconst, bcv, bpw, bfg], writes=[bZ[zb]])
                P.op("dve", lambda e, b=b, zb=zb: e.tensor_copy(out=gbc[:, b, :], in_=Z[zb][:]),
                     reads=[bZ[zb]], writes=[bgbc])
            load_wo(0)

        def newton(ss_col, base, bnw, reads, iters):
            a = nwt[:, base:base + 1]
            y = nwt[:, base + 1:base + 2]
            P.op("dve", lambda e: e.tensor_scalar(out=a, in0=ss_col, scalar1=1.0 / D, scalar2=EPS,
                                                  op0=ALU.mult, op1=ALU.add), reads=reads, writes=[bnw])
            P.op("pool", lambda e: e.tensor_tensor(out=y, in0=a, in1=mhalf[:, 0:1], op=ALU.pow),
                 reads=[bnw, bconst], writes=[bnw])
            return y

        prep_cnt = {"n": 0}

        def load_x(p, t):
            gi = p * 8 + t
            if p >= NPASS:
                return
            s = gi % 2
            P.dma("sp", "xt%d" % s,
                  lambda e, s=s, gi=gi: e.dma_start(out=XT[s][:], in_=x_d[gi * 128:(gi + 1) * 128, :]),
                  writes=[bXT[s]])

        stage = {}

        def xsrc(gi):
            if gi in stage:
                return stage[gi]
            return XT[gi % 2], bXT[gi % 2]

        def prep_front(p, t):
            gi = p * 8 + t
            s = gi % 2
            xnb = XNB[s]
            xt, bxt = xsrc(gi)
            ssc = nwt[:, 8 * s + 3:8 * s + 4]
            P.op("act", lambda e: e.activation(out=xnb[:], in_=xt[:], func=AF.Square, accum_out=ssc),
                 reads=[bxt], writes=[bXNB[s], bnwp[s]])
            newton(ssc, 8 * s, bnwp[s], [bnwp[s]], 2)

        def prep_norm(p, t):
            gi = p * 8 + t
            s = gi % 2
            xnb = XNB[s]
            xt, bxt = xsrc(gi)
            y = nwt[:, 8 * s + 1:8 * s + 2]
            P.op("act", lambda e: e.activation(out=xnb[:], in_=xt[:], func=AF.Copy, scale=y),
                 reads=[bxt, bnwp[s]], writes=[bXNB[s]])
            nt_ = gi + 2
            if nt_ >= 8 and gi >= 6:
                load_x(nt_ // 8, nt_ % 8)

        def prep_back(p, t):
            gi = p * 8 + t
            s = gi % 2
            hb = p % 2
            b = (p * TP) // SEQ
            xnb = XNB[s]
            zb = next_z()
            PT = Z[zb][:, 0:512].bitcast(BF16).rearrange("p (f t) -> p f t", t=128)
            bPT = bZ[zb]

            def tr(e):
                ins = None
                for fc in range(8):
                    ins = e.transpose(PT[:, fc, :], xnb[:, fc * 128:(fc + 1) * 128], ident[:])
                return ins
            P.op("pe", tr, reads=[bXNB[s], bconst], writes=[bPT])

            def ev(e):
                ins = None
                for fc in range(8):
                    ins = e.activation(out=hT[hb][:, fc, t * 128:(t + 1) * 128], in_=PT[:, fc, :],
                                       func=AF.Identity, scale=Asc[:, b, fc:fc + 1], bias=mod[:, fc, b:b + 1])
                return ins
            P.op("act", ev, reads=[bPT, bmod], writes=[bhT[hb][t]])

        def prep_back_raw(t):
            s = t % 2
            xnb = XNB[s]
            zb = next_z()
            PT = Z[zb][:, 0:512].bitcast(BF16).rearrange("p (f t) -> p f t", t=128)

            def tr(e):
                ins = None
                for fc in range(8):
                    ins = e.transpose(PT[:, fc, :], xnb[:, fc * 128:(fc + 1) * 128], ident[:])
                return ins
            P.op("pe", tr, reads=[bXNB[s], bconst], writes=[bZ[zb]])
            P.op("act", lambda e: e.activation(out=hT[0][:, :, t * 128:(t + 1) * 128], in_=PT, func=AF.Copy),
                 reads=[bZ[zb]], writes=[bhT[0][t], bstart])

        zrot = {"n": 0}
        wpos = {"n": 0}

        def next_z():
            i = zrot["n"] % NZ
            zrot["n"] += 1
            return i

        def take_w(p, expect):
            n = wpos["n"]
            if n < NCH:
                order.append(expect)
            else:
                assert order[n % NCH] == expect, (order[n % NCH], expect)
            wpos["n"] += 1
            return n

        def release_w(n):
            issue_w(n + NSLOT)

        def z_mm(p, zc):
            n = take_w(p, ("z", zc))
            slot = n % NSLOT
            zb = next_z()
            hb = p % 2
            w = wsl[slot]

            def mm(e):
                ins = None
                for k in range(8):
                    for sblk in range(2):
                        ins = e.matmul(Z[zb][:, sblk * 512:(sblk + 1) * 512], w[:, k, :],
                                       hT[hb][:, k, sblk * 512:(sblk + 1) * 512], start=(k == 0), stop=(k == 7))
                return ins
            P.op("pe", mm, reads=[bws[slot]] + bhT[hb], writes=[bZ[zb]])
            release_w(n)
            return zb

        def o_mm(p, kind, j, ysb, bysb):
            n = take_w(p, (kind, j))
            slot = n % NSLOT
            zb = next_z()
            w = wsl[slot]

            def mm(e):
                ins = None
                for k in range(8):
                    for sblk in range(2):
                        ins = e.matmul(Z[zb][:, sblk * 512:(sblk + 1) * 512], w[:, k, :],
                                       ysb[:, k, sblk * 512:(sblk + 1) * 512], start=(k == 0), stop=(k == 7))
                return ins
            P.op("pe", mm, reads=[bws[slot]] + bysb, writes=[bZ[zb]])
            release_w(n)
            return zb

        tcnt = {"n": 0}

        def conv_chunk(p, j, hook1=None, hook2=None):
            half = p % 2
            ti = tcnt["n"] % 2
            tcnt["n"] += 1
            t1, t2, t3, v = T1[ti], T2[ti], T3[ti], V[ti]
            if half == 0:
                P.op("dve", lambda e: e.memset(v[:, 0:2], 0.0), writes=[bV[ti]])
            else:
                P.op("dve", lambda e: e.tensor_copy(out=v[:, 0:2], in_=Vh[:, j, :]), reads=[bVh[j]], writes=[bV[ti]])
            zC = z_mm(p, 24 + j)
            P.op("act", lambda e: e.activation(out=t1[:], in_=Z[zC][:], func=AF.Identity, bias=c_bin(24 + j), scale=1.0),
                 reads=[bZ[zC], bconst, bcv, bpw, bfg], writes=[bT1[ti]])
            if hook1 is not None:
                hook1()
            zv = z_mm(p, 32 + j)
            P.op("dve", lambda e: e.scalar_tensor_tensor(out=v[:, 2:TP + 2], in0=Z[zv][:], scalar=c_bin(32 + j),
                                                         in1=t1[:], op0=ALU.add, op1=ALU.mult),
                 reads=[bZ[zv], bT1[ti], bconst, bcv, bpw, bfg], writes=[bV[ti]])
            if half == 0:
                P.op("dve", lambda e: e.tensor_copy(out=Vh[:, j, :], in_=v[:, TP:TP + 2]), reads=[bV[ti]],
                     writes=[bVh[j]])
            zg = z_mm(p, 40 + j)
            P.op("act", lambda e: e.activation(out=t2[:], in_=Z[zg][:], func=AF.Silu, bias=c_bin(40 + j), scale=1.0),
                 reads=[bZ[zg], bconst, bcv, bpw, bfg], writes=[bT2[ti]])
            if hook2 is not None:
                hook2()
            P.op("act", lambda e: e.activation(out=t3[:], in_=v[:, 2:TP + 2], func=AF.Identity, scale=c_cw(j, 2),
                                               bias=c_cb(j)),
                 reads=[bV[ti], bconst, bcv, bpw, bfg], writes=[bT3[ti]])
            P.op("dve", lambda e: e.scalar_tensor_tensor(out=t3[:], in0=v[:, 1:TP + 1], scalar=c_cw(j, 1), in1=t3[:],
                                                         op0=ALU.mult, op1=ALU.add),
                 reads=[bV[ti], bT3[ti], bconst, bcv, bpw, bfg], writes=[bT3[ti]])
            P.op("dve", lambda e: e.scalar_tensor_tensor(out=t3[:], in0=v[:, 0:TP], scalar=c_cw(j, 0), in1=t3[:],
                                                         op0=ALU.mult, op1=ALU.add),
                 reads=[bV[ti], bT3[ti], bconst, bcv, bpw, bfg], writes=[bT3[ti]])
            zB = z_mm(p, 16 + j)
            P.op("dve", lambda e: e.scalar_tensor_tensor(out=t3[:], in0=Z[zB][:], scalar=c_bin(16 + j), in1=t3[:],
                                                         op0=ALU.add, op1=ALU.mult),
                 reads=[bZ[zB], bT3[ti], bconst, bcv, bpw, bfg], writes=[bT3[ti]])
            P.op("dve", lambda e: e.tensor_tensor(out=yb[:, j, :], in0=t3[:], in1=t2[:], op=ALU.mult),
                 reads=[bT3[ti], bT2[ti]], writes=[byb[j]])

        def pool_A(p, g):
            half = p % 2
            for ci in range(2):
                c = 2 * g + ci
                u = U[ci]
                if half == 0:
                    P.op("dve", lambda e, u=u: e.memset(u[:, 0:16], 0.0), writes=[bU[ci]])
                else:
                    P.op("dve", lambda e, u=u, c=c: e.tensor_copy(out=u[:, 0:16], in_=Uh[:, c, :]), reads=[bUh[c]],
                         writes=[bU[ci]])
                zb = z_mm(p, c)
                P.op("act", lambda e, u=u, zb=zb, c=c: e.activation(out=u[:, 16:TP + 16], in_=Z[zb][:], func=AF.Identity,
                                                                   bias=c_bin(c), scale=1.0),
                     reads=[bZ[zb], bconst, bcv, bpw, bfg], writes=[bU[ci]])
                if half == 0:
                    P.op("dve", lambda e, u=u, c=c: e.tensor_copy(out=Uh[:, c, :], in_=u[:, TP:TP + 16]),
                         reads=[bU[ci]], writes=[bUh[c]])

        def pool_chain(p, g, ci):
            half = p % 2
            w = WINDOWS[g]
            nst = g + 1
            u = U[ci]
            src, bsrc = u, bU[ci]
            lo = 1
            bufs = [(SA, bSA), (SB, bSB)]
            for k in range(nst):
                sh = 1 << k
                dst, bdst = bufs[k % 2]
                nlo = lo + sh
                P.op("dve", lambda e, dst=dst, src=src, nlo=nlo, sh=sh: e.tensor_tensor(
                    out=dst[:, nlo:TP + 16], in0=src[:, nlo:TP + 16], in1=src[:, nlo - sh:TP + 16 - sh], op=ALU.add),
                    reads=[bsrc], writes=[bdst])
                src, bsrc, lo = dst, bdst, nlo
            assert lo == w
            P.op("dve", lambda e, src=src: e.scalar_tensor_tensor(
                out=R[ci][:], in0=src[:, 16:TP + 16], scalar=1.0 / w, in1=u[:, 16:TP + 16],
                op0=ALU.mult, op1=ALU.subtract),
                reads=[bsrc, bU[ci]], writes=[bR[ci]])
            if half == 0:
                P.op("dve", lambda e, src=src: e.tensor_tensor(out=tfix[:, 0:w - 1], in0=src[:, 16:16 + w - 1],
                                                               in1=invc[:, 0:w - 1], op=ALU.mult),
                     reads=[bsrc, bconst, bcv, bpw, bfg], writes=[btfix])
                P.op("dve", lambda e: e.tensor_tensor(out=R[ci][:, 0:w - 1], in0=tfix[:, 0:w - 1],
                                                      in1=u[:, 16:16 + w - 1], op=ALU.subtract),
                     reads=[btfix, bU[ci]], writes=[bR[ci]])

        def pool_B(p, g):
            for oi in range(2):
                c = 2 * g + oi
                ti = tcnt["n"] % 2
                tcnt["n"] += 1
                zg_ = z_mm(p, 8 + c)
                P.op("act", lambda e, zg_=zg_, c=c, ti=ti: e.activation(out=T1[ti][:], in_=Z[zg_][:], func=AF.Silu,
                                                                       bias=c_bin(8 + c), scale=1.0),
                     reads=[bZ[zg_], bconst, bcv, bpw, bfg], writes=[bT1[ti]])
                zb = next_z()

                def mm(e, zb=zb, oi=oi):
                    ins = None
                    for sblk in range(2):
                        for k in range(2):
                            ins = e.matmul(Z[zb][:, sblk * 512:(sblk + 1) * 512],
                                           pw[:, g, k, oi * 128:(oi + 1) * 128],
                                           R[k][:, sblk * 512:(sblk + 1) * 512], start=(k == 0), stop=(k == 1))
                    return ins
                P.op("pe", mm, reads=[bR[0], bR[1], bconst, bcv, bpw, bfg], writes=[bZ[zb]])
                P.op("dve", lambda e, zb=zb, c=c, ti=ti: e.scalar_tensor_tensor(
                    out=ya[:, c, :], in0=Z[zb][:], scalar=c_psc(c), in1=T1[ti][:], op0=ALU.mult, op1=ALU.mult),
                    reads=[bZ[zb], bT1[ti], bconst, bcv, bpw, bfg], writes=[bya[c]])

        def merge_chunk(p, j):
            ti = tcnt["n"] % 2
            tcnt["n"] += 1
            t1, t2, t3 = T1[ti], T2[ti], T3[ti]
            zoa = o_mm(p, "a", j, ya, bya)
            zma = z_mm(p, 48 + j)
            P.op("act", lambda e: e.activation(out=t1[:], in_=Z[zma][:], func=AF.Sigmoid, bias=c_bin(48 + j), scale=1.0),
                 reads=[bZ[zma], bconst, bcv, bpw, bfg], writes=[bT1[ti]])
            P.op("dve", lambda e: e.tensor_tensor(out=t3[:], in0=Z[zoa][:], in1=t1[:], op=ALU.mult),
                 reads=[bZ[zoa], bT1[ti]], writes=[bT3[ti]])
            zob = o_mm(p, "b", j, yb, byb)
            zmb = z_mm(p, 56 + j)
            P.op("act", lambda e: e.activation(out=t2[:], in_=Z[zmb][:], func=AF.Sigmoid, bias=c_bin(56 + j), scale=1.0),
                 reads=[bZ[zmb], bconst, bcv, bpw, bfg], writes=[bT2[ti]])
            P.op("dve", lambda e: e.tensor_tensor(out=t2[:], in0=Z[zob][:], in1=t2[:], op=ALU.mult),
                 reads=[bZ[zob], bT2[ti]], writes=[bT2[ti]])
            P.op("dve", lambda e: e.tensor_tensor(out=mg[:, j, :], in0=t3[:], in1=t2[:], op=ALU.add),
                 reads=[bT3[ti], bT2[ti]], writes=[bmg[j]])

        def load_wo(b):
            def dma_k(k):
                P.dma("pool", "swo%d" % k,
                      lambda e: e.dma_start(out=wo[:, k, :], in_=wo_d[:, k * 1024:(k + 1) * 1024]),
                      writes=[bwo[k]])

            def scale_k(k):
                P.op("pool", lambda e: e.tensor_tensor(out=wo[:, k, :], in0=wo[:, k, :], in1=gbc[:, b, :], op=ALU.mult),
                     reads=[bwo[k], bgbc], writes=[bwo[k]])
            items = [lambda: dma_k(0)]
            for k in range(8):
                if k + 1 < 8:
                    items.append(lambda k=k: dma_k(k + 1))
                items.append(lambda k=k: scale_k(k))
            pool_work.extend(items)

        def tail_xload(p, t):
            gi = p * 8 + t
            s = gi % NXN
            P.dma("sp", "xl%d" % s,
                  lambda e: e.dma_start(out=XN[s][:], in_=x_d[gi * 128:(gi + 1) * 128, :]),
                  writes=[bXN[s]])

        def tail_mm(p, t, pre=False):
            gi = p * 8 + t
            s = gi % NXN
            xn = XN[s]
            zb = next_z()

            def mm(e):
                ins = None
                for dh in range(2):
                    for k in range(8):
                        ins = e.matmul(Z[zb][:, dh * 512:(dh + 1) * 512], mg[:, k, t * 128:(t + 1) * 128],
                                       wo[:, k, dh * 512:(dh + 1) * 512], start=(k == 0), stop=(k == 7))
                return ins
            if t == 0:
                def mm_a(e):
                    ins = None
                    for dh in range(2):
                        for k in range(7):
                            ins = e.matmul(Z[zb][:, dh * 512:(dh + 1) * 512], mg[:, k, t * 128:(t + 1) * 128],
                                           wo[:, k, dh * 512:(dh + 1) * 512], start=(k == 0), stop=False)
                    return ins

                def mm_b(e):
                    ins = None
                    for dh in range(2):
                        ins = e.matmul(Z[zb][:, dh * 512:(dh + 1) * 512], mg[:, 7, t * 128:(t + 1) * 128],
                                       wo[:, 7, dh * 512:(dh + 1) * 512], start=False, stop=True)
                    return ins
                P.op("pe", mm_a, reads=bmg[0:7] + bwo, writes=[bZ[zb]])
                P.op("pe", mm_b, reads=bmg + bwo, writes=[bZ[zb]])
            else:
                P.op("pe", mm, reads=bmg + bwo, writes=[bZ[zb]])
            if pre:
                P.op("dve", lambda e: e.tensor_tensor(out=xn[:], in0=Z[zb][:], in1=xn[:], op=ALU.add),
                     reads=[bZ[zb], bXN[s]], writes=[bXN[s]])
                return
            P.op("act", lambda e: e.activation(out=xn[:], in_=Z[zb][:], func=AF.Copy),
                 reads=[bZ[zb]], writes=[bXN[s]])
            P.dma("pool", "xa%d" % s,
                  lambda e: e.dma_start(out=xn[:], in_=x_d[gi * 128:(gi + 1) * 128, :], accum_op=ALU.add),
                  reads=[bXN[s]], writes=[bXN[s]])

        def tail_fa(p, t):
            gi = p * 8 + t
            s = gi % NXN
            q = gi % 2
            xn = XN[s]
            ssc = nwt[:, 16 + 8 * q + 3:16 + 8 * q + 4]
            P.op("act", lambda e: e.activation(out=junk[:], in_=xn[:], func=AF.Square, accum_out=ssc),
                 reads=[bXN[s]], writes=[bjunk, bnwt[q]])
            newton(ssc, 16 + 8 * q, bnwt[q], [bnwt[q]], 3)

        def tail_fb(p, t):
            gi = p * 8 + t
            s = gi % NXN
            q = gi % 2
            xn = XN[s]
            y = nwt[:, 16 + 8 * q + 1:16 + 8 * q + 2]
            P.op("dve", lambda e: e.scalar_tensor_tensor(out=xn[:], in0=xn[:], scalar=y, in1=fgbc[:],
                                                         op0=ALU.mult, op1=ALU.mult),
                 reads=[bXN[s], bnwt[q], bconst, bcv, bpw, bfg], writes=[bXN[s]])
            P.dma("sp", "out%d" % s,
                  lambda e: e.dma_start(out=y_d[gi * 128:(gi + 1) * 128, :], in_=xn[:]),
                  reads=[bXN[s]])

        load_x(0, 0)
        load_x(0, 1)
        stg = [(T1[0], bT1[0]), (T1[1], bT1[1]), (T2[0], bT2[0]), (T2[1], bT2[1]), (T3[0], bT3[0]), (T3[1], bT3[1])]
        for gi in range(2, 8):
            buf, bb = stg[gi - 2]
            stage[gi] = (buf, bb)
            P.dma("sp", "xs%d" % gi,
                  lambda e, buf=buf, gi=gi: e.dma_start(out=buf[:], in_=x_d[gi * 128:(gi + 1) * 128, :]),
                  writes=[bb])
        setup()
        prep_front(0, 0)
        for t in range(8):
            if t + 1 < 8:
                prep_front(0, t + 1)
            prep_norm(0, t)
            prep_back_raw(t)
            if t == 4:
                setup_b1()
        setup_b2()
        def tail_step(pp, i):
            last = (pp == NPASS - 1)
            if 0 <= i < 8:
                tail_mm(pp, i, pre=last)
            if 0 <= i - 1 < 8:
                tail_fa(pp, i - 1)
            if 0 <= i - 2 < 8:
                tail_fb(pp, i - 2)
                if last and i - 2 + NXN < 8:
                    tail_xload(pp, i - 2 + NXN)

        def seq_of(pp):
            return (pp * TP) // SEQ

        for p in range(NPASS):
            nxt = p + 1 < NPASS
            for i in range(8):
                g = i // 2
                if nxt and i == 6:
                    prep_front(p + 1, 0)
                if nxt and i == 7:
                    prep_norm(p + 1, 0)
                    prep_front(p + 1, 1)
                if i % 2 == 0:
                    pool_A(p, g)
                    conv_chunk(p, i, hook1=lambda: pool_chain(p, g, 0), hook2=lambda: pool_chain(p, g, 1))
                else:
                    pool_B(p, g)
                    conv_chunk(p, i)
                if p == 0 and i == 1:
                    setup_gate()
                if p > 0:
                    tail_step(p - 1, i)
                drain_pool_work(2)
            if p > 0:
                tail_step(p - 1, 8)
                tail_step(p - 1, 9)
                if seq_of(p) != seq_of(p - 1):
                    load_wo(seq_of(p))
            if nxt:
                prep_norm(p + 1, 1)
                prep_back(p + 1, 0)
            else:
                for t in range(NXN):
                    tail_xload(p, t)
            for j in range(8):
                if nxt and j + 2 < 8:
                    prep_front(p + 1, j + 2)
                merge_chunk(p, j)
                if nxt:
                    if j + 2 < 8:
                        prep_norm(p + 1, j + 2)
                    if j + 1 < 8:
                        prep_back(p + 1, j + 1)
                drain_pool_work(2)
            drain_pool_work(16)
        for i in range(10):
            tail_step(NPASS - 1, i)
        P.final_wait("sp")
        P.emit()
    assert len(order) == NCH
    return nc, order


def _chunk_layout(w, c):
    blk = w[:, c * 128:(c + 1) * 128].reshape(8, 128, 128)
    return np.ascontiguousarray(blk.transpose(1, 0, 2)).reshape(128, 1024)


def _host_layout(inputs, nseq, ncores, order):
    f = lambda a: np.ascontiguousarray(np.asarray(a, dtype=np.float32))
    x = f(inputs["x"]); c = f(inputs["c"])
    ada_w = f(inputs["ada_w"])[0]; ada_b = f(inputs["ada_b"])[0]; norm_g = f(inputs["norm_g"])[0]
    w_in = f(inputs["w_in"])[0]; b_in = f(inputs["b_in"])[0]; pool_w = f(inputs["pool_w"])[0]
    pool_scale = f(inputs["pool_scale"])[0]; conv_w = f(inputs["conv_w"])[0]; conv_b = f(inputs["conv_b"])[0]
    w_out_a = f(inputs["w_out_a"])[0]; w_out_b = f(inputs["w_out_b"])[0]; w_o = f(inputs["w_o"])[0]
    final_g = f(inputs["final_g"])

    ws = np.empty((NCH, 128, 1024), np.float32)
    for n, (kind, idx) in enumerate(order):
        src = {"z": w_in, "a": w_out_a, "b": w_out_b}[kind]
        ws[n] = _chunk_layout(src, idx)
    adaw = np.empty((3, 128, 8192), np.float32)
    for i in range(3):
        piece = ada_w[:, i * 1024:(i + 1) * 1024].reshape(8, 128, 1024)
        adaw[i] = piece.transpose(1, 0, 2).reshape(128, 8192)
    wo = np.ascontiguousarray(w_o.reshape(8, 128, 1024).transpose(1, 0, 2)).reshape(128, 8192)
    pw = np.ascontiguousarray(pool_w.reshape(4, 2, 128, 256).transpose(2, 0, 1, 3)).reshape(128, 2048)
    cvec = np.empty((128, 136), np.float32)
    cvec[:, 0:24] = ada_b.reshape(24, 128).T
    cvec[:, 24:32] = norm_g.reshape(8, 128).T
    cvec[:, 32:96] = b_in.reshape(64, 128).T
    cvec[:, 96:104] = pool_scale.reshape(8, 128).T
    cvec[:, 104:128] = conv_w.reshape(3, 8, 128).transpose(2, 1, 0).reshape(128, 24)
    cvec[:, 128:136] = conv_b.reshape(8, 128).T
    bcv = np.stack([ada_b[2048:3072], final_g]).astype(np.float32)

    in_maps = []
    for i in range(ncores):
        xs = x[i * nseq:(i + 1) * nseq].reshape(nseq * SEQ, D)
        cs = c[i * nseq:(i + 1) * nseq]
        cT = np.ascontiguousarray(cs.reshape(nseq, 8, 128).transpose(2, 1, 0)).reshape(128, 8 * nseq)
        in_maps.append({"x": np.ascontiguousarray(xs), "cT": cT, "ws": ws, "adaw": adaw, "wo": wo, "pw": pw,
                        "cvec": cvec, "bcv": bcv})
    return in_maps


_NC_CACHE = {}


def kernel(**inputs):
    B = np.asarray(inputs["x"]).shape[0]
    nseq = B // NCORES
    if nseq not in _NC_CACHE:
        _NC_CACHE[nseq] = build_nc(nseq)
    nc, order = _NC_CACHE[nseq]
    in_maps = _host_layout(inputs, nseq, NCORES, order)
    res = run_bass_kernel_spmd(nc, in_maps, core_ids=list(range(NCORES)))
    out = np.stack([r["y"].reshape(nseq, SEQ, D) for r in res.results], axis=0)
    return out.reshape(B, SEQ, D).astype(np.float32)
```

```python
import numpy as np
from contextlib import ExitStack
import concourse.bass as bass
import concourse.mybir as mybir
from concourse.bass_utils import run_bass_kernel_spmd

F32 = mybir.dt.float32
BF16 = mybir.dt.bfloat16
I32 = mybir.dt.int32
AF = mybir.ActivationFunctionType
ALU = mybir.AluOpType

D = 1024
SEQ = 2048
NCORES = 8
TP = 1024
NCH = 80
NSLOT = 5
EPS = 1e-6
WINDOWS = (2, 4, 8, 16)


class Buf:
    __slots__ = ("name", "w", "r")

    def __init__(self, name):
        self.name = name
        self.w = None
        self.r = {}


class Prog:
    ENGS = ("pe", "act", "dve", "pool", "sp")

    def __init__(self, nc, stack):
        self.nc = nc
        self.stack = stack
        self.q = {e: [] for e in self.ENGS}
        self.cnt = {e: 0 for e in self.ENGS}
        self.known = {e: {} for e in self.ENGS}
        self.esem = {}
        for e in ("pe", "act", "dve", "pool"):
            self.esem[e] = stack.enter_context(nc.semaphore("s_" + e))
        self.dsem = {}
        self.dcnt = {}

    def _waits(self, eng, reads, writes):
        need = {}

        def add(tok):
            if tok is None:
                return
            s, v = tok
            if eng == "pe" and s == ("e", "pe"):
                return
            if need.get(s, 0) < v:
                need[s] = v

        for b in reads:
            add(b.w)
        for b in writes:
            add(b.w)
            for t in b.r.items():
                add(t)
        out = []
        kn = self.known[eng]
        for s, v in need.items():
            if kn.get(s, 0) >= v:
                continue
            kn[s] = v
            out.append((s, v))
        return out

    def _mark(self, tok, reads, writes):
        for b in reads:
            if b.r.get(tok[0], 0) < tok[1]:
                b.r[tok[0]] = tok[1]
        for b in writes:
            b.w = tok
            b.r = {}

    def op(self, eng, fn, reads=(), writes=()):
        waits = self._waits(eng, reads, writes)
        self.cnt[eng] += 1
        tok = (("e", eng), self.cnt[eng])
        self.q[eng].append((fn, waits, ("e", eng), 1))
        self._mark(tok, reads, writes)
        return tok

    def dma(self, eng, semname, fn, reads=(), writes=()):
        if semname not in self.dsem:
            self.dsem[semname] = self.stack.enter_context(self.nc.semaphore("d_" + semname))
            self.dcnt[semname] = 0
        waits = self._waits(eng, reads, writes)
        prev = self.dcnt[semname]
        s = ("d", semname)
        if prev > 0 and self.known[eng].get(s, 0) < prev:
            self.known[eng][s] = prev
            waits.append((s, prev))
        self.dcnt[semname] += 16
        tok = (s, self.dcnt[semname])
        self.q[eng].append((fn, waits, s, 16))
        self._mark(tok, reads, writes)
        return tok

    def final_wait(self, eng):
        waits = [(("d", n), v) for n, v in self.dcnt.items() if v > 0]
        self.q[eng].append((None, waits, None, 0))

    def _sem(self, s):
        kind, name = s
        return self.esem[name] if kind == "e" else self.dsem[name]

    def emit(self):
        nc = self.nc
        with nc.Block() as block:
            def run(engname):
                def body(e):
                    for fn, waits, s, inc in self.q[engname]:
                        for (ws, wv) in waits:
                            e.wait_ge(self._sem(ws), wv)
                        if fn is None:
                            continue
                        ins = fn(e)
                        ins.then_inc(self._sem(s), inc)
                return body

            block.tensor(run("pe"))
            block.scalar(run("act"))
            block.vector(run("dve"))
            block.gpsimd(run("pool"))
            block.sync(run("sp"))


def build_nc(nseq):
    NT = nseq * SEQ
    NPASS = NT // TP
    nc = bass.Bass("TRN2", target_bir_lowering=False)
    dt_in = lambda n, sh: nc.dram_tensor(n, sh, F32, kind="ExternalInput").ap()
    x_d = dt_in("x", [NT, D])
    cT_d = dt_in("cT", [128, 8 * nseq])
    ws_d = dt_in("ws", [NCH, 128, 1024])
    adaw_d = dt_in("adaw", [3, 128, 8192])
    wo_d = dt_in("wo", [128, 8192])
    pw_d = dt_in("pw", [128, 2048])
    cv_d = dt_in("cvec", [128, 136])
    bc_d = dt_in("bcv", [2, 1024])
    y_d = nc.dram_tensor("y", [NT, D], F32, kind="ExternalOutput").ap()

    with ExitStack() as st:
        P = Prog(nc, st)
        sbt = lambda n, sh, dt: st.enter_context(nc.sbuf_tensor("sb_" + n, sh, dt))
        pst = lambda n, sh, dt: st.enter_context(nc.psum_tensor("ps_" + n, sh, dt))

        hT = [sbt("hT%d" % i, [128, 8, TP], BF16) for i in range(2)]
        ya = sbt("ya", [128, 8, TP], BF16)
        yb = sbt("yb", [128, 8, TP], BF16)
        mg = sbt("mg", [128, 8, TP], BF16)
        wsl = [sbt("wsl%d" % i, [128, 8, 128], BF16) for i in range(NSLOT)]
        pw = sbt("pw", [128, 4, 2, 256], BF16)
        wo = sbt("wo", [128, 8, 1024], BF16)
        gbc = sbt("gbc", [128, nseq, 1024], BF16)
        fgbc = sbt("fgbc", [128, 1024], F32)
        cvec = sbt("cvec", [128, 136], F32)
        ident = sbt("ident", [128, 128], BF16)
        cT = sbt("cT", [128, 8 * nseq], F32)
        scb = sbt("scb", [128, 8, nseq], BF16)
        mod = sbt("mod", [128, 24, nseq], F32)
        Asc = sbt("Asc", [128, nseq, 8], F32)
        gbt = sbt("gbt", [128, 2, 128], BF16)
        invc = sbt("invc", [128, 16], F32)
        mhalf = sbt("mhalf", [128, 1], F32)
        invi = sbt("invi", [128, 16], I32)
        XT = [sbt("xt%d" % i, [128, 1024], F32) for i in range(2)]
        XNB = [sbt("xnb%d" % i, [128, 1024], BF16) for i in range(2)]
        junk = sbt("junk", [128, 1024], BF16)
        NXN = 4
        XN = [sbt("xn%d" % i, [128, 1024], F32) for i in range(NXN)]
        nwt = sbt("nwt", [128, 32], F32)
        T1 = [sbt("t1_%d" % i, [128, TP], F32) for i in range(2)]
        T2 = [sbt("t2_%d" % i, [128, TP], F32) for i in range(2)]
        T3 = [sbt("t3_%d" % i, [128, TP], F32) for i in range(2)]
        V = [sbt("v_%d" % i, [128, TP + 2], F32) for i in range(2)]
        U = [sbt("u_%d" % i, [128, TP + 16], F32) for i in range(2)]
        SA = sbt("sa", [128, TP + 16], F32)
        SB = sbt("sb", [128, TP + 16], F32)
        R = [sbt("r_%d" % i, [128, TP], BF16) for i in range(2)]
        Vh = sbt("vh", [128, 8, 2], F32)
        Uh = sbt("uh", [128, 8, 16], F32)
        tfix = sbt("tfix", [128, 16], F32)

        NZ = 4
        Z = [pst("z%d" % i, [128, 1024], F32) for i in range(NZ)]
        PS = Z[0][:, 0:24 * nseq].rearrange("p (m b) -> p m b", b=nseq)

        bZ = [Buf("z%d" % i) for i in range(4)]
        bPS = bZ[0]
        bhT = [[Buf("hT%d_%d" % (i, t)) for t in range(8)] for i in range(2)]
        bya = [Buf("ya%d" % k) for k in range(8)]
        byb = [Buf("yb%d" % k) for k in range(8)]
        bmg = [Buf("mg%d" % k) for k in range(8)]
        bws = [Buf("ws%d" % i) for i in range(NSLOT)]
        bconst = Buf("const")
        bcT = Buf("cT")
        bscb = Buf("scb")
        bfg = Buf("fg")
        bpw = Buf("pw")
        bcv = Buf("cvec")
        bwo = [Buf("wo%d" % k) for k in range(8)]
        bmodg = Buf("modg")
        bmod = Buf("mod")
        bstart = Buf("start")
        bgbc = Buf("gbc")
        bXT = [Buf("xt%d" % i) for i in range(2)]
        bXNB = [Buf("xnb%d" % i) for i in range(2)]
        bjunk = Buf("junk")
        bXN = [Buf("xn%d" % i) for i in range(4)]
        bnwp = [Buf("nwp0"), Buf("nwp1")]
        bnwt = [Buf("nwt0"), Buf("nwt1")]
        bT1 = [Buf("t1_%d" % i) for i in range(2)]
        bT2 = [Buf("t2_%d" % i) for i in range(2)]
        bT3 = [Buf("t3_%d" % i) for i in range(2)]
        bV = [Buf("v%d" % i) for i in range(2)]
        bU = [Buf("u%d" % i) for i in range(2)]
        bSA = Buf("sa")
        bSB = Buf("sb")
        bR = [Buf("r%d" % i) for i in range(2)]
        bVh = [Buf("vh%d" % k) for k in range(8)]
        bUh = [Buf("uh%d" % k) for k in range(8)]
        btfix = Buf("tfix")
        bgbt = [Buf("gbt%d" % i) for i in range(2)]

        def c_adab(m):
            return cvec[:, m:m + 1]

        def c_bin(zc):
            return cvec[:, 32 + zc:33 + zc]

        def c_psc(k):
            return cvec[:, 96 + k:97 + k]

        def c_cw(k, j):
            return cvec[:, 104 + 3 * k + j:105 + 3 * k + j]

        def c_cb(k):
            return cvec[:, 128 + k:129 + k]

        order = []
        total_chunks = NPASS * NCH
        wstate = {"issued": 0}

        def issue_w(n):
            if n >= total_chunks:
                return
            slot = n % NSLOT
            src = ws_d[n % NCH]
            dst = wsl[slot]
            P.dma("pool", "w%d" % slot,
                  lambda e, dst=dst, src=src: e.dma_start(out=dst[:].rearrange("p k n -> p (k n)"), in_=src),
                  writes=[bws[slot]])

        def setup():
            P.op("pool", lambda e: e.memset(mhalf[:], -0.5), writes=[bconst])
            P.op("pool", lambda e: e.memset(ident[:], 0.0), writes=[bconst])
            P.op("pool", lambda e: e.affine_select(out=ident[:], in_=ident[:], pattern=[[-1, 128]],
                                                   compare_op=ALU.not_equal, fill=1.0, base=0, channel_multiplier=1),
                 reads=[bconst], writes=[bconst])
            P.op("pool", lambda e: e.iota(invi[:], pattern=[[1, 16]], base=1, channel_multiplier=0),
                 writes=[btfix])
            P.dma("sp", "s0", lambda e: e.dma_start(out=cvec[:], in_=cv_d), writes=[bcv])
            P.dma("sp", "s1", lambda e: e.dma_start(out=cT[:], in_=cT_d), writes=[bcT])
            P.dma("sp", "s3", lambda e: e.dma_start(out=fgbc[:], in_=bc_d[1:2, :].to_broadcast([128, 1024])),
                  writes=[bfg])
            for i in (0, 1):
                P.dma("pool", "sa%d" % i,
                      lambda e, i=i: e.dma_start(out=pieces[i][:].rearrange("p k n -> p (k n)"), in_=adaw_d[i]),
                      writes=[bpieces[i]])
            for n in range(NSLOT):
                issue_w(n)
            P.op("dve", lambda e: e.tensor_copy(out=invc[:], in_=invi[:]), reads=[btfix], writes=[bconst])
            P.op("dve", lambda e: e.reciprocal(out=invc[:], in_=invc[:]), reads=[bconst], writes=[bconst])

        def setup_b1():
            P.op("act", lambda e: e.activation(out=scb[:].rearrange("p k b -> p (k b)"), in_=cT[:], func=AF.Silu),
                 reads=[bcT], writes=[bscb])
            zb = next_z()
            psv = Z[zb][:, 0:24 * nseq].rearrange("p (m b) -> p m b", b=nseq)
            for i in (0, 1):
                P.op("pe", lambda e, i=i: ada_mm(e, i, psv), reads=[bpieces[i], bscb], writes=[bZ[zb]])
            P.op("dve", lambda e: e.tensor_tensor(out=mod[:, 0:16, :], in0=psv[:, 0:16, :],
                                                  in1=cvec[:, 0:16].unsqueeze(2).to_broadcast([128, 16, nseq]),
                                                  op=ALU.add),
                 reads=[bZ[zb], bconst, bcv, bpw, bfg], writes=[bmod])
            for b in range(nseq):
                P.op("dve", lambda e, b=b: e.scalar_tensor_tensor(out=Asc[:, b, :], in0=mod[:, 8:16, b], scalar=1.0,
                                                                  in1=cvec[:, 24:32], op0=ALU.add, op1=ALU.mult),
                     reads=[bmod, bconst, bcv, bpw, bfg], writes=[bmod])

        def setup_b2():
            P.dma("pool", "spw", lambda e: e.dma_start(out=pw[:].rearrange("p g k n -> p (g k n)"), in_=pw_d),
                  reads=[bstart], writes=[bpw])
            P.dma("pool", "sa2",
                  lambda e: e.dma_start(out=pieces[2][:].rearrange("p k n -> p (k n)"), in_=adaw_d[2]),
                  reads=[bstart], writes=[bpieces[2]])
            for fc in range(8):
                P.op("dve", lambda e, fc=fc: e.tensor_scalar(out=hT[0][:, fc, :], in0=hT[0][:, fc, :],
                                                             scalar1=Asc[:, 0, fc:fc + 1], scalar2=mod[:, fc, 0:1],
                                                             op0=ALU.mult, op1=ALU.add),
                     reads=[bmod] + bhT[0], writes=bhT[0])

        pieces = [ya, yb, mg]
        bpieces = [bya[0], byb[0], bmg[0]]

        def ada_mm(e, i, ps):
            ins = None
            for ml in range(8):
                for k in range(8):
                    ins = e.matmul(ps[:, i * 8 + ml, :], pieces[i][:, k, ml * 128:(ml + 1) * 128],
                                   scb[:, k, :], start=(k == 0), stop=(k == 7))
            return ins

        pool_work = []

        def drain_pool_work(n=1):
            for _ in range(n):
                if pool_work:
                    pool_work.pop(0)()

        def setup_gate():
            zb = next_z()
            psv = Z[zb][:, 0:24 * nseq].rearrange("p (m b) -> p m b", b=nseq)
            P.op("pe", lambda e: ada_mm(e, 2, psv), reads=[bpieces[2], bscb], writes=[bZ[zb]])
            P.op("dve", lambda e: e.tensor_tensor(out=mod[:, 16:24, :], in0=psv[:, 16:24, :],
                                                  in1=cvec[:, 16:24].unsqueeze(2).to_broadcast([128, 8, nseq]),
                                                  op=ALU.add),
                 reads=[bZ[zb], bconst, bcv, bpw, bfg], writes=[bmodg])
            for b in range(nseq):
                zb = next_z()
                for k in range(8):
                    s_ = (b * 8 + k) % 2
                    P.op("dve", lambda e, s_=s_, k=k, b=b: e.tensor_copy(
                        out=gbt[:, s_, :], in_=mod[:, 16 + k, b:b + 1].to_broadcast([128, 128])),
                        reads=[bmodg], writes=[bgbt[s_]])
                    P.op("pe", lambda e, s_=s_, k=k, zb=zb: e.matmul(Z[zb][:, k * 128:(k + 1) * 128], gbt[:, s_, :],
                                                                    ident[:], start=True, stop=True),
                         reads=[bgbt[s_], bconst, bcv, bpw, bfg], writes=[bZ[zb]])
                P.op("dve", lambda e, b=b, zb=zb: e.tensor_copy(out=gbc[:, b, :], in_=Z[zb][:]),
                     reads=[bZ[zb]], writes=[bgbc])
            load_wo(0)

        def newton(ss_col, base, bnw, reads, iters):
            a = nwt[:, base:base + 1]
            y = nwt[:, base + 1:base + 2]
            P.op("dve", lambda e: e.tensor_scalar(out=a, in0=ss_col, scalar1=1.0 / D, scalar2=EPS,
                                                  op0=ALU.mult, op1=ALU.add), reads=reads, writes=[bnw])
            P.op("pool", lambda e: e.tensor_tensor(out=y, in0=a, in1=mhalf[:, 0:1], op=ALU.pow),
                 reads=[bnw, bconst], writes=[bnw])
            return y

        prep_cnt = {"n": 0}

        def load_x(p, t):
            gi = p * 8 + t
            if p >= NPASS:
                return
            s = gi % 2
            P.dma("sp", "xt%d" % s,
                  lambda e, s=s, gi=gi: e.dma_start(out=XT[s][:], in_=x_d[gi * 128:(gi + 1) * 128, :]),
                  writes=[bXT[s]])

        stage = {}

        def xsrc(gi):
            if gi in stage:
                return stage[gi]
            return XT[gi % 2], bXT[gi % 2]

        def prep_front(p, t):
            gi = p * 8 + t
            s = gi % 2
            xnb = XNB[s]
            xt, bxt = xsrc(gi)
            ssc = nwt[:, 8 * s + 3:8 * s + 4]
            P.op("act", lambda e: e.activation(out=xnb[:], in_=xt[:], func=AF.Square, accum_out=ssc),
                 reads=[bxt], writes=[bXNB[s], bnwp[s]])
            newton(ssc, 8 * s, bnwp[s], [bnwp[s]], 2)

        def prep_norm(p, t):
            gi = p * 8 + t
            s = gi % 2
            xnb = XNB[s]
            xt, bxt = xsrc(gi)
            y = nwt[:, 8 * s + 1:8 * s + 2]
            P.op("act", lambda e: e.activation(out=xnb[:], in_=xt[:], func=AF.Copy, scale=y),
                 reads=[bxt, bnwp[s]], writes=[bXNB[s]])
            nt_ = gi + 2
            if nt_ >= 8 and gi >= 6:
                load_x(nt_ // 8, nt_ % 8)

        def prep_back(p, t):
            gi = p * 8 + t
            s = gi % 2
            hb = p % 2
            b = (p * TP) // SEQ
            xnb = XNB[s]
            zb = next_z()
            PT = Z[zb][:, 0:512].bitcast(BF16).rearrange("p (f t) -> p f t", t=128)
            bPT = bZ[zb]

            def tr(e):
                ins = None
                for fc in range(8):
                    ins = e.transpose(PT[:, fc, :], xnb[:, fc * 128:(fc + 1) * 128], ident[:])
                return ins
            P.op("pe", tr, reads=[bXNB[s], bconst], writes=[bPT])

            def ev(e):
                ins = None
                for fc in range(8):
                    ins = e.activation(out=hT[hb][:, fc, t * 128:(t + 1) * 128], in_=PT[:, fc, :],
                                       func=AF.Identity, scale=Asc[:, b, fc:fc + 1], bias=mod[:, fc, b:b + 1])
                return ins
            P.op("act", ev, reads=[bPT, bmod], writes=[bhT[hb][t]])

        def prep_back_raw(t):
            s = t % 2
            xnb = XNB[s]
            zb = next_z()
            PT = Z[zb][:, 0:512].bitcast(BF16).rearrange("p (f t) -> p f t", t=128)

            def tr(e):
                ins = None
                for fc in range(8):
                    ins = e.transpose(PT[:, fc, :], xnb[:, fc * 128:(fc + 1) * 128], ident[:])
                return ins
            P.op("pe", tr, reads=[bXNB[s], bconst], writes=[bZ[zb]])
            P.op("act", lambda e: e.activation(out=hT[0][:, :, t * 128:(t + 1) * 128], in_=PT, func=AF.Copy),
                 reads=[bZ[zb]], writes=[bhT[0][t], bstart])

        zrot = {"n": 0}
        wpos = {"n": 0}

        def next_z():
            i = zrot["n"] % NZ
            zrot["n"] += 1
            return i

        def take_w(p, expect):
            n = wpos["n"]
            if n < NCH:
                order.append(expect)
            else:
                assert order[n % NCH] == expect, (order[n % NCH], expect)
            wpos["n"] += 1
            return n

        def release_w(n):
            issue_w(n + NSLOT)

        def z_mm(p, zc):
            n = take_w(p, ("z", zc))
            slot = n % NSLOT
            zb = next_z()
            hb = p % 2
            w = wsl[slot]

            def mm(e):
                ins = None
                for k in range(8):
                    for sblk in range(2):
                        ins = e.matmul(Z[zb][:, sblk * 512:(sblk + 1) * 512], w[:, k, :],
                                       hT[hb][:, k, sblk * 512:(sblk + 1) * 512], start=(k == 0), stop=(k == 7))
                return ins
            P.op("pe", mm, reads=[bws[slot]] + bhT[hb], writes=[bZ[zb]])
            release_w(n)
            return zb

        def o_mm(p, kind, j, ysb, bysb):
            n = take_w(p, (kind, j))
            slot = n % NSLOT
            zb = next_z()
            w = wsl[slot]

            def mm(e):
                ins = None
                for k in range(8):
                    for sblk in range(2):
                        ins = e.matmul(Z[zb][:, sblk * 512:(sblk + 1) * 512], w[:, k, :],
                                       ysb[:, k, sblk * 512:(sblk + 1) * 512], start=(k == 0), stop=(k == 7))
                return ins
            P.op("pe", mm, reads=[bws[slot]] + bysb, writes=[bZ[zb]])
            release_w(n)
            return zb

        tcnt = {"n": 0}

        def conv_chunk(p, j, hook1=None, hook2=None):
            half = p % 2
            ti = tcnt["n"] % 2
            tcnt["n"] += 1
            t1, t2, t3, v = T1[ti], T2[ti], T3[ti], V[ti]
            if half == 0:
                P.op("dve", lambda e: e.memset(v[:, 0:2], 0.0), writes=[bV[ti]])
            else:
                P.op("dve", lambda e: e.tensor_copy(out=v[:, 0:2], in_=Vh[:, j, :]), reads=[bVh[j]], writes=[bV[ti]])
            zC = z_mm(p, 24 + j)
            P.op("act", lambda e: e.activation(out=t1[:], in_=Z[zC][:], func=AF.Identity, bias=c_bin(24 + j), scale=1.0),
                 reads=[bZ[zC], bconst, bcv, bpw, bfg], writes=[bT1[ti]])
            if hook1 is not None:
                hook1()
            zv = z_mm(p, 32 + j)
            P.op("dve", lambda e: e.scalar_tensor_tensor(out=v[:, 2:TP + 2], in0=Z[zv][:], scalar=c_bin(32 + j),
                                                         in1=t1[:], op0=ALU.add, op1=ALU.mult),
                 reads=[bZ[zv], bT1[ti], bconst, bcv, bpw, bfg], writes=[bV[ti]])
            if half == 0:
                P.op("dve", lambda e: e.tensor_copy(out=Vh[:, j, :], in_=v[:, TP:TP + 2]), reads=[bV[ti]],
                     writes=[bVh[j]])
            zg = z_mm(p, 40 + j)
            P.op("act", lambda e: e.activation(out=t2[:], in_=Z[zg][:], func=AF.Silu, bias=c_bin(40 + j), scale=1.0),
                 reads=[bZ[zg], bconst, bcv, bpw, bfg], writes=[bT2[ti]])
            if hook2 is not None:
                hook2()
            P.op("act", lambda e: e.activation(out=t3[:], in_=v[:, 2:TP + 2], func=AF.Identity, scale=c_cw(j, 2),
                                               bias=c_cb(j)),
                 reads=[bV[ti], bconst, bcv, bpw, bfg], writes=[bT3[ti]])
            P.op("dve", lambda e: e.scalar_tensor_tensor(out=t3[:], in0=v[:, 1:TP + 1], scalar=c_cw(j, 1), in1=t3[:],
                                                         op0=ALU.mult, op1=ALU.add),
                 reads=[bV[ti], bT3[ti], bconst, bcv, bpw, bfg], writes=[bT3[ti]])
            P.op("dve", lambda e: e.scalar_tensor_tensor(out=t3[:], in0=v[:, 0:TP], scalar=c_cw(j, 0), in1=t3[:],
                                                         op0=ALU.mult, op1=ALU.add),
                 reads=[bV[ti], bT3[ti], bconst, bcv, bpw, bfg], writes=[bT3[ti]])
            zB = z_mm(p, 16 + j)
            P.op("dve", lambda e: e.scalar_tensor_tensor(out=t3[:], in0=Z[zB][:], scalar=c_bin(16 + j), in1=t3[:],
                                                         op0=ALU.add, op1=ALU.mult),
                 reads=[bZ[zB], bT3[ti], bconst, bcv, bpw, bfg], writes=[bT3[ti]])
            P.op("dve", lambda e: e.tensor_tensor(out=yb[:, j, :], in0=t3[:], in1=t2[:], op=ALU.mult),
                 reads=[bT3[ti], bT2[ti]], writes=[byb[j]])

        def pool_A(p, g):
            half = p % 2
            for ci in range(2):
                c = 2 * g + ci
                u = U[ci]
                if half == 0:
                    P.op("dve", lambda e, u=u: e.memset(u[:, 0:16], 0.0), writes=[bU[ci]])
                else:
                    P.op("dve", lambda e, u=u, c=c: e.tensor_copy(out=u[:, 0:16], in_=Uh[:, c, :]), reads=[bUh[c]],
                         writes=[bU[ci]])
                zb = z_mm(p, c)
                P.op("act", lambda e, u=u, zb=zb, c=c: e.activation(out=u[:, 16:TP + 16], in_=Z[zb][:], func=AF.Identity,
                                                                   bias=c_bin(c), scale=1.0),
                     reads=[bZ[zb], bconst, bcv, bpw, bfg], writes=[bU[ci]])
                if half == 0:
                    P.op("dve", lambda e, u=u, c=c: e.tensor_copy(out=Uh[:, c, :], in_=u[:, TP:TP + 16]),
                         reads=[bU[ci]], writes=[bUh[c]])

        def pool_chain(p, g, ci):
            half = p % 2
            w = WINDOWS[g]
            nst = g + 1
            u = U[ci]
            src, bsrc = u, bU[ci]
            lo = 1
            bufs = [(SA, bSA), (SB, bSB)]
            for k in range(nst):
                sh = 1 << k
                dst, bdst = bufs[k % 2]
                nlo = lo + sh
                P.op("dve", lambda e, dst=dst, src=src, nlo=nlo, sh=sh: e.tensor_tensor(
                    out=dst[:, nlo:TP + 16], in0=src[:, nlo:TP + 16], in1=src[:, nlo - sh:TP + 16 - sh], op=ALU.add),
                    reads=[bsrc], writes=[bdst])
                src, bsrc, lo = dst, bdst, nlo
            assert lo == w
            P.op("dve", lambda e, src=src: e.scalar_tensor_tensor(
                out=R[ci][:], in0=src[:, 16:TP + 16], scalar=1.0 / w, in1=u[:, 16:TP + 16],
                op0=ALU.mult, op1=ALU.subtract),
                reads=[bsrc, bU[ci]], writes=[bR[ci]])
            if half == 0:
                P.op("dve", lambda e, src=src: e.tensor_tensor(out=tfix[:, 0:w - 1], in0=src[:, 16:16 + w - 1],
                                                               in1=invc[:, 0:w - 1], op=ALU.mult),
                     reads=[bsrc, bconst, bcv, bpw, bfg], writes=[btfix])
                P.op("dve", lambda e: e.tensor_tensor(out=R[ci][:, 0:w - 1], in0=tfix[:, 0:w - 1],
                                                      in1=u[:, 16:16 + w - 1], op=ALU.subtract),
                     reads=[btfix, bU[ci]], writes=[bR[ci]])

        def pool_B(p, g):
            for oi in range(2):
                c = 2 * g + oi
                ti = tcnt["n"] % 2
                tcnt["n"] += 1
                zg_ = z_mm(p, 8 + c)
                P.op("act", lambda e, zg_=zg_, c=c, ti=ti: e.activation(out=T1[ti][:], in_=Z[zg_][:], func=AF.Silu,
                                                                       bias=c_bin(8 + c), scale=1.0),
                     reads=[bZ[zg_], bconst, bcv, bpw, bfg], writes=[bT1[ti]])
                zb = next_z()

                def mm(e, zb=zb, oi=oi):
                    ins = None
                    for sblk in range(2):
                        for k in range(2):
                            ins = e.matmul(Z[zb][:, sblk * 512:(sblk + 1) * 512],
                                           pw[:, g, k, oi * 128:(oi + 1) * 128],
                                           R[k][:, sblk * 512:(sblk + 1) * 512], start=(k == 0), stop=(k == 1))
                    return ins
                P.op("pe", mm, reads=[bR[0], bR[1], bconst, bcv, bpw, bfg], writes=[bZ[zb]])
                P.op("dve", lambda e, zb=zb, c=c, ti=ti: e.scalar_tensor_tensor(
                    out=ya[:, c, :], in0=Z[zb][:], scalar=c_psc(c), in1=T1[ti][:], op0=ALU.mult, op1=ALU.mult),
                    reads=[bZ[zb], bT1[ti], bconst, bcv, bpw, bfg], writes=[bya[c]])

        def merge_chunk(p, j):
            ti = tcnt["n"] % 2
            tcnt["n"] += 1
            t1, t2, t3 = T1[ti], T2[ti], T3[ti]
            zoa = o_mm(p, "a", j, ya, bya)
            zma = z_mm(p, 48 + j)
            P.op("act", lambda e: e.activation(out=t1[:], in_=Z[zma][:], func=AF.Sigmoid, bias=c_bin(48 + j), scale=1.0),
                 reads=[bZ[zma], bconst, bcv, bpw, bfg], writes=[bT1[ti]])
            P.op("dve", lambda e: e.tensor_tensor(out=t3[:], in0=Z[zoa][:], in1=t1[:], op=ALU.mult),
                 reads=[bZ[zoa], bT1[ti]], writes=[bT3[ti]])
            zob = o_mm(p, "b", j, yb, byb)
            zmb = z_mm(p, 56 + j)
            P.op("act", lambda e: e.activation(out=t2[:], in_=Z[zmb][:], func=AF.Sigmoid, bias=c_bin(56 + j), scale=1.0),
                 reads=[bZ[zmb], bconst, bcv, bpw, bfg], writes=[bT2[ti]])
            P.op("dve", lambda e: e.tensor_tensor(out=t2[:], in0=Z[zob][:], in1=t2[:], op=ALU.mult),
                 reads=[bZ[zob], bT2[ti]], writes=[bT2[ti]])
            P.op("dve", lambda e: e.tensor_tensor(out=mg[:, j, :], in0=t3[:], in1=t2[:], op=ALU.add),
                 reads=[bT3[ti], bT2[ti]], writes=[bmg[j]])

        def load_wo(b):
            def dma_k(k):
                P.dma("pool", "swo%d" % k,
                      lambda e: e.dma_start(out=wo[:, k, :], in_=wo_d[:, k * 1024:(k + 1) * 1024]),
                      writes=[bwo[k]])

            def scale_k(k):
                P.op("pool", lambda e: e.tensor_tensor(out=wo[:, k, :], in0=wo[:, k, :], in1=gbc[:, b, :], op=ALU.mult),
                     reads=[bwo[k], bgbc], writes=[bwo[k]])
            items = [lambda: dma_k(0)]
            for k in range(8):
                if k + 1 < 8:
                    items.append(lambda k=k: dma_k(k + 1))
                items.append(lambda k=k: scale_k(k))
            pool_work.extend(items)

        def tail_xload(p, t):
            gi = p * 8 + t
            s = gi % NXN
            P.dma("sp", "xl%d" % s,
                  lambda e: e.dma_start(out=XN[s][:], in_=x_d[gi * 128:(gi + 1) * 128, :]),
                  writes=[bXN[s]])

        def tail_mm(p, t, pre=False):
            gi = p * 8 + t
            s = gi % NXN
            xn = XN[s]
            zb = next_z()

            def mm(e):
                ins = None
                for dh in range(2):
                    for k in range(8):
                        ins = e.matmul(Z[zb][:, dh * 512:(dh + 1) * 512], mg[:, k, t * 128:(t + 1) * 128],
                                       wo[:, k, dh * 512:(dh + 1) * 512], start=(k == 0), stop=(k == 7))
                return ins
            if t == 0:
                def mm_a(e):
                    ins = None
                    for dh in range(2):
                        for k in range(7):
                            ins = e.matmul(Z[zb][:, dh * 512:(dh + 1) * 512], mg[:, k, t * 128:(t + 1) * 128],
                                           wo[:, k, dh * 512:(dh + 1) * 512], start=(k == 0), stop=False)
                    return ins

                def mm_b(e):
                    ins = None
                    for dh in range(2):
                        ins = e.matmul(Z[zb][:, dh * 512:(dh + 1) * 512], mg[:, 7, t * 128:(t + 1) * 128],
                                       wo[:, 7, dh * 512:(dh + 1) * 512], start=False, stop=True)
                    return ins
                P.op("pe", mm_a, reads=bmg[0:7] + bwo, writes=[bZ[zb]])
                P.op("pe", mm_b, reads=bmg + bwo, writes=[bZ[zb]])
            else:
                P.op("pe", mm, reads=bmg + bwo, writes=[bZ[zb]])
            if pre:
                P.op("dve", lambda e: e.tensor_tensor(out=xn[:], in0=Z[zb][:], in1=xn[:], op=ALU.add),
                     reads=[bZ[zb], bXN[s]], writes=[bXN[s]])
                return
            P.op("act", lambda e: e.activation(out=xn[:], in_=Z[zb][:], func=AF.Copy),
                 reads=[bZ[zb]], writes=[bXN[s]])
            P.dma("pool", "xa%d" % s,
                  lambda e: e.dma_start(out=xn[:], in_=x_d[gi * 128:(gi + 1) * 128, :], accum_op=ALU.add),
                  reads=[bXN[s]], writes=[bXN[s]])

        def tail_fa(p, t):
            gi = p * 8 + t
            s = gi % NXN
            q = gi % 2
            xn = XN[s]
            ssc = nwt[:, 16 + 8 * q + 3:16 + 8 * q + 4]
            P.op("act", lambda e: e.activation(out=junk[:], in_=xn[:], func=AF.Square, accum_out=ssc),
                 reads=[bXN[s]], writes=[bjunk, bnwt[q]])
            newton(ssc, 16 + 8 * q, bnwt[q], [bnwt[q]], 3)

        def tail_fb(p, t):
            gi = p * 8 + t
            s = gi % NXN
            q = gi % 2
            xn = XN[s]
            y = nwt[:, 16 + 8 * q + 1:16 + 8 * q + 2]
            P.op("dve", lambda e: e.scalar_tensor_tensor(out=xn[:], in0=xn[:], scalar=y, in1=fgbc[:],
                                                         op0=ALU.mult, op1=ALU.mult),
                 reads=[bXN[s], bnwt[q], bconst, bcv, bpw, bfg], writes=[bXN[s]])
            P.dma("sp", "out%d" % s,
                  lambda e: e.dma_start(out=y_d[gi * 128:(gi + 1) * 128, :], in_=xn[:]),
                  reads=[bXN[s]])

        load_x(0, 0)
        load_x(0, 1)
        stg = [(T1[0], bT1[0]), (T1[1], bT1[1]), (T2[0], bT2[0]), (T2[1], bT2[1]), (T3[0], bT3[0]), (T3[1], bT3[1])]
        for gi in range(2, 8):
            buf, bb = stg[gi - 2]
            stage[gi] = (buf, bb)
            P.dma("sp", "xs%d" % gi,
                  lambda e, buf=buf, gi=gi: e.dma_start(out=buf[:], in_=x_d[gi * 128:(gi + 1) * 128, :]),
                  writes=[bb])
        setup()
        prep_front(0, 0)
        for t in range(8):
            if t + 1 < 8:
                prep_front(0, t + 1)
            prep_norm(0, t)
            prep_back_raw(t)
            if t == 4:
                setup_b1()
        setup_b2()
        def tail_step(pp, i):
            if 0 <= i < 8:
                tail_mm(pp, i, pre=True)
            if 0 <= i - 1 < 8:
                tail_fa(pp, i - 1)
            if 0 <= i - 2 < 8:
                tail_fb(pp, i - 2)
                if i - 2 + NXN < 8:
                    tail_xload(pp, i - 2 + NXN)

        def seq_of(pp):
            return (pp * TP) // SEQ

        for p in range(NPASS):
            nxt = p + 1 < NPASS
            for i in range(8):
                g = i // 2
                if nxt and i == 6:
                    prep_front(p + 1, 0)
                if nxt and i == 7:
                    prep_norm(p + 1, 0)
                    prep_front(p + 1, 1)
                if i % 2 == 0:
                    pool_A(p, g)
                    conv_chunk(p, i, hook1=lambda: pool_chain(p, g, 0), hook2=lambda: pool_chain(p, g, 1))
                else:
                    pool_B(p, g)
                    conv_chunk(p, i)
                if p == 0 and i == 1:
                    setup_gate()
                if p > 0:
                    tail_step(p - 1, i)
                drain_pool_work(2)
            if p > 0:
                tail_step(p - 1, 8)
                tail_step(p - 1, 9)
                if seq_of(p) != seq_of(p - 1):
                    load_wo(seq_of(p))
            for t in range(NXN):
                tail_xload(p, t)
            if nxt:
                prep_norm(p + 1, 1)
                prep_back(p + 1, 0)
            for j in range(8):
                if nxt and j + 2 < 8:
                    prep_front(p + 1, j + 2)
                merge_chunk(p, j)
                if nxt:
                    if j + 2 < 8:
                        prep_norm(p + 1, j + 2)
                    if j + 1 < 8:
                        prep_back(p + 1, j + 1)
                drain_pool_work(2)
            drain_pool_work(16)
        for i in range(10):
            tail_step(NPASS - 1, i)
        P.final_wait("sp")
        P.emit()
    assert len(order) == NCH
    return nc, order


def _chunk_layout(w, c):
    blk = w[:, c * 128:(c + 1) * 128].reshape(8, 128, 128)
    return np.ascontiguousarray(blk.transpose(1, 0, 2)).reshape(128, 1024)


def _host_layout(inputs, nseq, ncores, order):
    f = lambda a: np.ascontiguousarray(np.asarray(a, dtype=np.float32))
    x = f(inputs["x"]); c = f(inputs["c"])
    ada_w = f(inputs["ada_w"])[0]; ada_b = f(inputs["ada_b"])[0]; norm_g = f(inputs["norm_g"])[0]
    w_in = f(inputs["w_in"])[0]; b_in = f(inputs["b_in"])[0]; pool_w = f(inputs["pool_w"])[0]
    pool_scale = f(inputs["pool_scale"])[0]; conv_w = f(inputs["conv_w"])[0]; conv_b = f(inputs["conv_b"])[0]
    w_out_a = f(inputs["w_out_a"])[0]; w_out_b = f(inputs["w_out_b"])[0]; w_o = f(inputs["w_o"])[0]
    final_g = f(inputs["final_g"])

    ws = np.empty((NCH, 128, 1024), np.float32)
    for n, (kind, idx) in enumerate(order):
        src = {"z": w_in, "a": w_out_a, "b": w_out_b}[kind]
        ws[n] = _chunk_layout(src, idx)
    adaw = np.empty((3, 128, 8192), np.float32)
    for i in range(3):
        piece = ada_w[:, i * 1024:(i + 1) * 1024].reshape(8, 128, 1024)
        adaw[i] = piece.transpose(1, 0, 2).reshape(128, 8192)
    wo = np.ascontiguousarray(w_o.reshape(8, 128, 1024).transpose(1, 0, 2)).reshape(128, 8192)
    pw = np.ascontiguousarray(pool_w.reshape(4, 2, 128, 256).transpose(2, 0, 1, 3)).reshape(128, 2048)
    cvec = np.empty((128, 136), np.float32)
    cvec[:, 0:24] = ada_b.reshape(24, 128).T
    cvec[:, 24:32] = norm_g.reshape(8, 128).T
    cvec[:, 32:96] = b_in.reshape(64, 128).T
    cvec[:, 96:104] = pool_scale.reshape(8, 128).T
    cvec[:, 104:128] = conv_w.reshape(3, 8, 128).transpose(2, 1, 0).reshape(128, 24)
    cvec[:, 128:136] = conv_b.reshape(8, 128).T
    bcv = np.stack([ada_b[2048:3072], final_g]).astype(np.float32)

    in_maps = []
    for i in range(ncores):
        xs = x[i * nseq:(i + 1) * nseq].reshape(nseq * SEQ, D)
        cs = c[i * nseq:(i + 1) * nseq]
        cT = np.ascontiguousarray(cs.reshape(nseq, 8, 128).transpose(2, 1, 0)).reshape(128, 8 * nseq)
        in_maps.append({"x": np.ascontiguousarray(xs), "cT": cT, "ws": ws, "adaw": adaw, "wo": wo, "pw": pw,
                        "cvec": cvec, "bcv": bcv})
    return in_maps


_NC_CACHE = {}


def kernel(**inputs):
    B = np.asarray(inputs["x"]).shape[0]
    nseq = B // NCORES
    if nseq not in _NC_CACHE:
        _NC_CACHE[nseq] = build_nc(nseq)
    nc, order = _NC_CACHE[nseq]
    in_maps = _host_layout(inputs, nseq, NCORES, order)
    res = run_bass_kernel_spmd(nc, in_maps, core_ids=list(range(NCORES)))
    out = np.stack([r["y"].reshape(nseq, SEQ, D) for r in res.results], axis=0)
    return out.reshape(B, SEQ, D).astype(np.float32)
```

```python
import numpy as np
from contextlib import ExitStack
import concourse.bass as bass
import concourse.mybir as mybir
from concourse.bass_utils import run_bass_kernel_spmd

F32 = mybir.dt.float32
BF16 = mybir.dt.bfloat16
I32 = mybir.dt.int32
AF = mybir.ActivationFunctionType
ALU = mybir.AluOpType

D = 1024
SEQ = 2048
NCORES = 8
TP = 1024
NCH = 80
NSLOT = 5
EPS = 1e-6
WINDOWS = (2, 4, 8, 16)


class Buf:
    __slots__ = ("name", "w", "r")

    def __init__(self, name):
        self.name = name
        self.w = None
        self.r = {}


class Prog:
    ENGS = ("pe", "act", "dve", "pool", "sp")

    def __init__(self, nc, stack):
        self.nc = nc
        self.stack = stack
        self.q = {e: [] for e in self.ENGS}
        self.cnt = {e: 0 for e in self.ENGS}
        self.known = {e: {} for e in self.ENGS}
        self.esem = {}
        for e in ("pe", "act", "dve", "pool"):
            self.esem[e] = stack.enter_context(nc.semaphore("s_" + e))
        self.dsem = {}
        self.dcnt = {}

    def _waits(self, eng, reads, writes):
        need = {}

        def add(tok):
            if tok is None:
                return
            s, v = tok
            if eng == "pe" and s == ("e", "pe"):
                return
            if need.get(s, 0) < v:
                need[s] = v

        for b in reads:
            add(b.w)
        for b in writes:
            add(b.w)
            for t in b.r.items():
                add(t)
        out = []
        kn = self.known[eng]
        for s, v in need.items():
            if kn.get(s, 0) >= v:
                continue
            kn[s] = v
            out.append((s, v))
        return out

    def _mark(self, tok, reads, writes):
        for b in reads:
            if b.r.get(tok[0], 0) < tok[1]:
                b.r[tok[0]] = tok[1]
        for b in writes:
            b.w = tok
            b.r = {}

    def op(self, eng, fn, reads=(), writes=()):
        waits = self._waits(eng, reads, writes)
        self.cnt[eng] += 1
        tok = (("e", eng), self.cnt[eng])
        self.q[eng].append((fn, waits, ("e", eng), 1))
        self._mark(tok, reads, writes)
        return tok

    def dma(self, eng, semname, fn, reads=(), writes=()):
        if semname not in self.dsem:
            self.dsem[semname] = self.stack.enter_context(self.nc.semaphore("d_" + semname))
            self.dcnt[semname] = 0
        waits = self._waits(eng, reads, writes)
        prev = self.dcnt[semname]
        s = ("d", semname)
        if prev > 0 and self.known[eng].get(s, 0) < prev:
            self.known[eng][s] = prev
            waits.append((s, prev))
        self.dcnt[semname] += 16
        tok = (s, self.dcnt[semname])
        self.q[eng].append((fn, waits, s, 16))
        self._mark(tok, reads, writes)
        return tok

    def final_wait(self, eng):
        waits = [(("d", n), v) for n, v in self.dcnt.items() if v > 0]
        self.q[eng].append((None, waits, None, 0))

    def _sem(self, s):
        kind, name = s
        return self.esem[name] if kind == "e" else self.dsem[name]

    def emit(self):
        nc = self.nc
        with nc.Block() as block:
            def run(engname):
                def body(e):
                    for fn, waits, s, inc in self.q[engname]:
                        for (ws, wv) in waits:
                            e.wait_ge(self._sem(ws), wv)
                        if fn is None:
                            continue
                        ins = fn(e)
                        ins.then_inc(self._sem(s), inc)
                return body

            block.tensor(run("pe"))
            block.scalar(run("act"))
            block.vector(run("dve"))
            block.gpsimd(run("pool"))
            block.sync(run("sp"))


def build_nc(nseq):
    NT = nseq * SEQ
    NPASS = NT // TP
    nc = bass.Bass("TRN2", target_bir_lowering=False)
    dt_in = lambda n, sh: nc.dram_tensor(n, sh, F32, kind="ExternalInput").ap()
    x_d = dt_in("x", [NT, D])
    cT_d = dt_in("cT", [128, 8 * nseq])
    ws_d = dt_in("ws", [NCH, 128, 1024])
    adaw_d = dt_in("adaw", [3, 128, 8192])
    wo_d = dt_in("wo", [128, 8192])
    pw_d = dt_in("pw", [128, 2048])
    cv_d = dt_in("cvec", [128, 136])
    bc_d = dt_in("bcv", [2, 1024])
    y_d = nc.dram_tensor("y", [NT, D], F32, kind="ExternalOutput").ap()

    with ExitStack() as st:
        P = Prog(nc, st)
        sbt = lambda n, sh, dt: st.enter_context(nc.sbuf_tensor("sb_" + n, sh, dt))
        pst = lambda n, sh, dt: st.enter_context(nc.psum_tensor("ps_" + n, sh, dt))

        hT = [sbt("hT%d" % i, [128, 8, TP], BF16) for i in range(2)]
        ya = sbt("ya", [128, 8, TP], BF16)
        yb = sbt("yb", [128, 8, TP], BF16)
        mg = sbt("mg", [128, 8, TP], BF16)
        wsl = [sbt("wsl%d" % i, [128, 8, 128], BF16) for i in range(NSLOT)]
        pw = sbt("pw", [128, 4, 2, 256], BF16)
        wo = sbt("wo", [128, 8, 1024], BF16)
        gbc = sbt("gbc", [128, nseq, 1024], BF16)
        fgbc = sbt("fgbc", [128, 1024], F32)
        cvec = sbt("cvec", [128, 136], F32)
        ident = sbt("ident", [128, 128], BF16)
        cT = sbt("cT", [128, 8 * nseq], F32)
        scb = sbt("scb", [128, 8, nseq], BF16)
        mod = sbt("mod", [128, 24, nseq], F32)
        Asc = sbt("Asc", [128, nseq, 8], F32)
        gbt = sbt("gbt", [128, 2, 128], BF16)
        invc = sbt("invc", [128, 16], F32)
        mhalf = sbt("mhalf", [128, 1], F32)
        invi = sbt("invi", [128, 16], I32)
        XT = [sbt("xt%d" % i, [128, 1024], F32) for i in range(2)]
        XNB = [sbt("xnb%d" % i, [128, 1024], BF16) for i in range(2)]
        junk = sbt("junk", [128, 1024], BF16)
        NXN = 4
        XN = [sbt("xn%d" % i, [128, 1024], F32) for i in range(NXN)]
        nwt = sbt("nwt", [128, 32], F32)
        T1 = [sbt("t1_%d" % i, [128, TP], F32) for i in range(2)]
        T2 = [sbt("t2_%d" % i, [128, TP], F32) for i in range(2)]
        T3 = [sbt("t3_%d" % i, [128, TP], F32) for i in range(2)]
        V = [sbt("v_%d" % i, [128, TP + 2], F32) for i in range(2)]
        U = [sbt("u_%d" % i, [128, TP + 16], F32) for i in range(2)]
        SA = sbt("sa", [128, TP + 16], F32)
        SB = sbt("sb", [128, TP + 16], F32)
        R = [sbt("r_%d" % i, [128, TP], BF16) for i in range(2)]
        Vh = sbt("vh", [128, 8, 2], F32)
        Uh = sbt("uh", [128, 8, 16], F32)
        tfix = sbt("tfix", [128, 16], F32)

        NZ = 4
        Z = [pst("z%d" % i, [128, 1024], F32) for i in range(NZ)]
        PS = Z[0][:, 0:24 * nseq].rearrange("p (m b) -> p m b", b=nseq)

        bZ = [Buf("z%d" % i) for i in range(4)]
        bPS = bZ[0]
        bhT = [[Buf("hT%d_%d" % (i, t)) for t in range(8)] for i in range(2)]
        bya = [Buf("ya%d" % k) for k in range(8)]
        byb = [Buf("yb%d" % k) for k in range(8)]
        bmg = [Buf("mg%d" % k) for k in range(8)]
        bws = [Buf("ws%d" % i) for i in range(NSLOT)]
        bconst = Buf("const")
        bcT = Buf("cT")
        bscb = Buf("scb")
        bfg = Buf("fg")
        bpw = Buf("pw")
        bcv = Buf("cvec")
        bwo = [Buf("wo%d" % k) for k in range(8)]
        bmodg = Buf("modg")
        bmod = Buf("mod")
        bstart = Buf("start")
        bgbc = Buf("gbc")
        bXT = [Buf("xt%d" % i) for i in range(2)]
        bXNB = [Buf("xnb%d" % i) for i in range(2)]
        bjunk = Buf("junk")
        bXN = [Buf("xn%d" % i) for i in range(4)]
        bnwp = [Buf("nwp0"), Buf("nwp1")]
        bnwt = [Buf("nwt0"), Buf("nwt1")]
        bT1 = [Buf("t1_%d" % i) for i in range(2)]
        bT2 = [Buf("t2_%d" % i) for i in range(2)]
        bT3 = [Buf("t3_%d" % i) for i in range(2)]
        bV = [Buf("v%d" % i) for i in range(2)]
        bU = [Buf("u%d" % i) for i in range(2)]
        bSA = Buf("sa")
        bSB = Buf("sb")
        bR = [Buf("r%d" % i) for i in range(2)]
        bVh = [Buf("vh%d" % k) for k in range(8)]
        bUh = [Buf("uh%d" % k) for k in range(8)]
        btfix = Buf("tfix")
        bgbt = [Buf("gbt%d" % i) for i in range(2)]

        def c_adab(m):
            return cvec[:, m:m + 1]

        def c_bin(zc):
            return cvec[:, 32 + zc:33 + zc]

        def c_psc(k):
            return cvec[:, 96 + k:97 + k]

        def c_cw(k, j):
            return cvec[:, 104 + 3 * k + j:105 + 3 * k + j]

        def c_cb(k):
            return cvec[:, 128 + k:129 + k]

        order = []
        total_chunks = NPASS * NCH

        def issue_w(n):
            if n >= total_chunks:
                return
            slot = n % NSLOT
            src = ws_d[n % NCH]
            dst = wsl[slot]
            P.dma("pool", "w%d" % slot,
                  lambda e, dst=dst, src=src: e.dma_start(out=dst[:].rearrange("p k n -> p (k n)"), in_=src),
                  writes=[bws[slot]])

        def setup():
            P.op("pool", lambda e: e.memset(mhalf[:], -0.5), writes=[bconst])
            P.op("pool", lambda e: e.memset(ident[:], 0.0), writes=[bconst])
            P.op("pool", lambda e: e.affine_select(out=ident[:], in_=ident[:], pattern=[[-1, 128]],
                                                   compare_op=ALU.not_equal, fill=1.0, base=0, channel_multiplier=1),
                 reads=[bconst], writes=[bconst])
            P.op("pool", lambda e: e.iota(invi[:], pattern=[[1, 16]], base=1, channel_multiplier=0),
                 writes=[btfix])
            P.dma("sp", "s0", lambda e: e.dma_start(out=cvec[:], in_=cv_d), writes=[bcv])
            P.dma("sp", "s1", lambda e: e.dma_start(out=cT[:], in_=cT_d), writes=[bcT])
            P.dma("sp", "s3", lambda e: e.dma_start(out=fgbc[:], in_=bc_d[1:2, :].to_broadcast([128, 1024])),
                  writes=[bfg])
            for i in (0, 1):
                P.dma("pool", "sa%d" % i,
                      lambda e, i=i: e.dma_start(out=pieces[i][:].rearrange("p k n -> p (k n)"), in_=adaw_d[i]),
                      writes=[bpieces[i]])
            for n in range(NSLOT):
                issue_w(n)
            P.op("dve", lambda e: e.tensor_copy(out=invc[:], in_=invi[:]), reads=[btfix], writes=[bconst])
            P.op("dve", lambda e: e.reciprocal(out=invc[:], in_=invc[:]), reads=[bconst], writes=[bconst])

        def setup_b1():
            P.op("act", lambda e: e.activation(out=scb[:].rearrange("p k b -> p (k b)"), in_=cT[:], func=AF.Silu),
                 reads=[bcT], writes=[bscb])
            zb = next_z()
            psv = Z[zb][:, 0:24 * nseq].rearrange("p (m b) -> p m b", b=nseq)
            for i in (0, 1):
                P.op("pe", lambda e, i=i: ada_mm(e, i, psv), reads=[bpieces[i], bscb], writes=[bZ[zb]])
            P.op("dve", lambda e: e.tensor_tensor(out=mod[:, 0:16, :], in0=psv[:, 0:16, :],
                                                  in1=cvec[:, 0:16].unsqueeze(2).to_broadcast([128, 16, nseq]),
                                                  op=ALU.add),
                 reads=[bZ[zb], bconst, bcv, bpw, bfg], writes=[bmod])
            for b in range(nseq):
                P.op("dve", lambda e, b=b: e.scalar_tensor_tensor(out=Asc[:, b, :], in0=mod[:, 8:16, b], scalar=1.0,
                                                                  in1=cvec[:, 24:32], op0=ALU.add, op1=ALU.mult),
                     reads=[bmod, bconst, bcv, bpw, bfg], writes=[bmod])

        def setup_b2():
            P.dma("pool", "spw", lambda e: e.dma_start(out=pw[:].rearrange("p g k n -> p (g k n)"), in_=pw_d),
                  reads=[bstart], writes=[bpw])
            P.dma("pool", "sa2",
                  lambda e: e.dma_start(out=pieces[2][:].rearrange("p k n -> p (k n)"), in_=adaw_d[2]),
                  reads=[bstart], writes=[bpieces[2]])
            for fc in range(8):
                P.op("dve", lambda e, fc=fc: e.tensor_scalar(out=hT[0][:, fc, :], in0=hT[0][:, fc, :],
                                                             scalar1=Asc[:, 0, fc:fc + 1], scalar2=mod[:, fc, 0:1],
                                                             op0=ALU.mult, op1=ALU.add),
                     reads=[bmod] + bhT[0], writes=bhT[0])

        pieces = [ya, yb, mg]
        bpieces = [bya[0], byb[0], bmg[0]]

        def ada_mm(e, i, ps):
            ins = None
            for ml in range(8):
                for k in range(8):
                    ins = e.matmul(ps[:, i * 8 + ml, :], pieces[i][:, k, ml * 128:(ml + 1) * 128],
                                   scb[:, k, :], start=(k == 0), stop=(k == 7))
            return ins

        pool_work = []

        def drain_pool_work(n=1):
            for _ in range(n):
                if pool_work:
                    pool_work.pop(0)()

        def setup_gate():
            zb = next_z()
            psv = Z[zb][:, 0:24 * nseq].rearrange("p (m b) -> p m b", b=nseq)
            P.op("pe", lambda e: ada_mm(e, 2, psv), reads=[bpieces[2], bscb], writes=[bZ[zb]])
            P.op("dve", lambda e: e.tensor_tensor(out=mod[:, 16:24, :], in0=psv[:, 16:24, :],
                                                  in1=cvec[:, 16:24].unsqueeze(2).to_broadcast([128, 8, nseq]),
                                                  op=ALU.add),
                 reads=[bZ[zb], bconst, bcv, bpw, bfg], writes=[bmodg])
            for b in range(nseq):
                zb = next_z()
                for k in range(8):
                    s_ = (b * 8 + k) % 2
                    P.op("dve", lambda e, s_=s_, k=k, b=b: e.tensor_copy(
                        out=gbt[:, s_, :], in_=mod[:, 16 + k, b:b + 1].to_broadcast([128, 128])),
                        reads=[bmodg], writes=[bgbt[s_]])
                    P.op("pe", lambda e, s_=s_, k=k, zb=zb: e.matmul(Z[zb][:, k * 128:(k + 1) * 128], gbt[:, s_, :],
                                                                    ident[:], start=True, stop=True),
                         reads=[bgbt[s_], bconst, bcv, bpw, bfg], writes=[bZ[zb]])
                P.op("dve", lambda e, b=b, zb=zb: e.tensor_copy(out=gbc[:, b, :], in_=Z[zb][:]),
                     reads=[bZ[zb]], writes=[bgbc])
            load_wo(0)

        def newton(ss_col, base, bnw, reads, iters):
            a = nwt[:, base:base + 1]
            y = nwt[:, base + 1:base + 2]
            P.op("dve", lambda e: e.tensor_scalar(out=a, in0=ss_col, scalar1=1.0 / D, scalar2=EPS,
                                                  op0=ALU.mult, op1=ALU.add), reads=reads, writes=[bnw])
            P.op("pool", lambda e: e.tensor_tensor(out=y, in0=a, in1=mhalf[:, 0:1], op=ALU.pow),
                 reads=[bnw, bconst], writes=[bnw])
            return y


        def load_x(p, t):
            gi = p * 8 + t
            if p >= NPASS:
                return
            s = gi % 2
            P.dma("sp", "xt%d" % s,
                  lambda e, s=s, gi=gi: e.dma_start(out=XT[s][:], in_=x_d[gi * 128:(gi + 1) * 128, :]),
                  writes=[bXT[s]])

        stage = {}

        def xsrc(gi):
            if gi in stage:
                return stage[gi]
            return XT[gi % 2], bXT[gi % 2]

        def prep_front(p, t):
            gi = p * 8 + t
            s = gi % 2
            xnb = XNB[s]
            xt, bxt = xsrc(gi)
            ssc = nwt[:, 8 * s + 3:8 * s + 4]
            P.op("act", lambda e: e.activation(out=xnb[:], in_=xt[:], func=AF.Square, accum_out=ssc),
                 reads=[bxt], writes=[bXNB[s], bnwp[s]])
            newton(ssc, 8 * s, bnwp[s], [bnwp[s]], 2)

        def prep_norm(p, t):
            gi = p * 8 + t
            s = gi % 2
            xnb = XNB[s]
            xt, bxt = xsrc(gi)
            y = nwt[:, 8 * s + 1:8 * s + 2]
            P.op("act", lambda e: e.activation(out=xnb[:], in_=xt[:], func=AF.Copy, scale=y),
                 reads=[bxt, bnwp[s]], writes=[bXNB[s]])
            nt_ = gi + 2
            if nt_ >= 8 and gi >= 6:
                load_x(nt_ // 8, nt_ % 8)

        def prep_back(p, t):
            gi = p * 8 + t
            s = gi % 2
            hb = p % 2
            b = (p * TP) // SEQ
            xnb = XNB[s]
            zb = next_z()
            PT = Z[zb][:, 0:512].bitcast(BF16).rearrange("p (f t) -> p f t", t=128)
            bPT = bZ[zb]

            def tr(e):
                ins = None
                for fc in range(8):
                    ins = e.transpose(PT[:, fc, :], xnb[:, fc * 128:(fc + 1) * 128], ident[:])
                return ins
            P.op("pe", tr, reads=[bXNB[s], bconst], writes=[bPT])

            def ev(e):
                ins = None
                for fc in range(8):
                    ins = e.activation(out=hT[hb][:, fc, t * 128:(t + 1) * 128], in_=PT[:, fc, :],
                                       func=AF.Identity, scale=Asc[:, b, fc:fc + 1], bias=mod[:, fc, b:b + 1])
                return ins
            P.op("act", ev, reads=[bPT, bmod], writes=[bhT[hb][t]])

        def prep_back_raw(t):
            s = t % 2
            xnb = XNB[s]
            zb = next_z()
            PT = Z[zb][:, 0:512].bitcast(BF16).rearrange("p (f t) -> p f t", t=128)

            def tr(e):
                ins = None
                for fc in range(8):
                    ins = e.transpose(PT[:, fc, :], xnb[:, fc * 128:(fc + 1) * 128], ident[:])
                return ins
            P.op("pe", tr, reads=[bXNB[s], bconst], writes=[bZ[zb]])
            P.op("act", lambda e: e.activation(out=hT[0][:, :, t * 128:(t + 1) * 128], in_=PT, func=AF.Copy),
                 reads=[bZ[zb]], writes=[bhT[0][t], bstart])

        zrot = {"n": 0}
        wpos = {"n": 0}

        def next_z():
            i = zrot["n"] % NZ
            zrot["n"] += 1
            return i

        def take_w(p, expect):
            n = wpos["n"]
            if n < NCH:
                order.append(expect)
            else:
                assert order[n % NCH] == expect, (order[n % NCH], expect)
            wpos["n"] += 1
            return n

        def release_w(n):
            issue_w(n + NSLOT)

        def z_mm(p, zc):
            n = take_w(p, ("z", zc))
            slot = n % NSLOT
            zb = next_z()
            hb = p % 2
            w = wsl[slot]

            def mm(e):
                ins = None
                for k in range(8):
                    for sblk in range(2):
                        ins = e.matmul(Z[zb][:, sblk * 512:(sblk + 1) * 512], w[:, k, :],
                                       hT[hb][:, k, sblk * 512:(sblk + 1) * 512], start=(k == 0), stop=(k == 7))
                return ins
            P.op("pe", mm, reads=[bws[slot]] + bhT[hb], writes=[bZ[zb]])
            release_w(n)
            return zb

        def o_mm(p, kind, j, ysb, bysb):
            n = take_w(p, (kind, j))
            slot = n % NSLOT
            zb = next_z()
            w = wsl[slot]

            def mm(e):
                ins = None
                for k in range(8):
                    for sblk in range(2):
                        ins = e.matmul(Z[zb][:, sblk * 512:(sblk + 1) * 512], w[:, k, :],
                                       ysb[:, k, sblk * 512:(sblk + 1) * 512], start=(k == 0), stop=(k == 7))
                return ins
            P.op("pe", mm, reads=[bws[slot]] + bysb, writes=[bZ[zb]])
            release_w(n)
            return zb

        tcnt = {"n": 0}

        def conv_chunk(p, j, hook1=None, hook2=None):
            half = p % 2
            ti = tcnt["n"] % 2
            tcnt["n"] += 1
            t1, t2, t3, v = T1[ti], T2[ti], T3[ti], V[ti]
            if half == 0:
                P.op("dve", lambda e: e.memset(v[:, 0:2], 0.0), writes=[bV[ti]])
            else:
                P.op("dve", lambda e: e.tensor_copy(out=v[:, 0:2], in_=Vh[:, j, :]), reads=[bVh[j]], writes=[bV[ti]])
            zC = z_mm(p, 24 + j)
            P.op("act", lambda e: e.activation(out=t1[:], in_=Z[zC][:], func=AF.Identity, bias=c_bin(24 + j), scale=1.0),
                 reads=[bZ[zC], bconst, bcv, bpw, bfg], writes=[bT1[ti]])
            if hook1 is not None:
                hook1()
            zv = z_mm(p, 32 + j)
            P.op("dve", lambda e: e.scalar_tensor_tensor(out=v[:, 2:TP + 2], in0=Z[zv][:], scalar=c_bin(32 + j),
                                                         in1=t1[:], op0=ALU.add, op1=ALU.mult),
                 reads=[bZ[zv], bT1[ti], bconst, bcv, bpw, bfg], writes=[bV[ti]])
            if half == 0:
                P.op("dve", lambda e: e.tensor_copy(out=Vh[:, j, :], in_=v[:, TP:TP + 2]), reads=[bV[ti]],
                     writes=[bVh[j]])
            zg = z_mm(p, 40 + j)
            P.op("act", lambda e: e.activation(out=t2[:], in_=Z[zg][:], func=AF.Silu, bias=c_bin(40 + j), scale=1.0),
                 reads=[bZ[zg], bconst, bcv, bpw, bfg], writes=[bT2[ti]])
            if hook2 is not None:
                hook2()
            P.op("act", lambda e: e.activation(out=t3[:], in_=v[:, 2:TP + 2], func=AF.Identity, scale=c_cw(j, 2),
                                               bias=c_cb(j)),
                 reads=[bV[ti], bconst, bcv, bpw, bfg], writes=[bT3[ti]])
            P.op("dve", lambda e: e.scalar_tensor_tensor(out=t3[:], in0=v[:, 1:TP + 1], scalar=c_cw(j, 1), in1=t3[:],
                                                         op0=ALU.mult, op1=ALU.add),
                 reads=[bV[ti], bT3[ti], bconst, bcv, bpw, bfg], writes=[bT3[ti]])
            P.op("dve", lambda e: e.scalar_tensor_tensor(out=t3[:], in0=v[:, 0:TP], scalar=c_cw(j, 0), in1=t3[:],
                                                         op0=ALU.mult, op1=ALU.add),
                 reads=[bV[ti], bT3[ti], bconst, bcv, bpw, bfg], writes=[bT3[ti]])
            zB = z_mm(p, 16 + j)
            P.op("dve", lambda e: e.scalar_tensor_tensor(out=t3[:], in0=Z[zB][:], scalar=c_bin(16 + j), in1=t3[:],
                                                         op0=ALU.add, op1=ALU.mult),
                 reads=[bZ[zB], bT3[ti], bconst, bcv, bpw, bfg], writes=[bT3[ti]])
            P.op("dve", lambda e: e.tensor_tensor(out=yb[:, j, :], in0=t3[:], in1=t2[:], op=ALU.mult),
                 reads=[bT3[ti], bT2[ti]], writes=[byb[j]])

        def pool_A(p, g):
            half = p % 2
            for ci in range(2):
                c = 2 * g + ci
                u = U[ci]
                if half == 0:
                    P.op("dve", lambda e, u=u: e.memset(u[:, 0:16], 0.0), writes=[bU[ci]])
                else:
                    P.op("dve", lambda e, u=u, c=c: e.tensor_copy(out=u[:, 0:16], in_=Uh[:, c, :]), reads=[bUh[c]],
                         writes=[bU[ci]])
                zb = z_mm(p, c)
                P.op("act", lambda e, u=u, zb=zb, c=c: e.activation(out=u[:, 16:TP + 16], in_=Z[zb][:], func=AF.Identity,
                                                                   bias=c_bin(c), scale=1.0),
                     reads=[bZ[zb], bconst, bcv, bpw, bfg], writes=[bU[ci]])
                if half == 0:
                    P.op("dve", lambda e, u=u, c=c: e.tensor_copy(out=Uh[:, c, :], in_=u[:, TP:TP + 16]),
                         reads=[bU[ci]], writes=[bUh[c]])

        def pool_chain(p, g, ci):
            half = p % 2
            w = WINDOWS[g]
            nst = g + 1
            u = U[ci]
            src, bsrc = u, bU[ci]
            lo = 1
            bufs = [(SA, bSA), (SB, bSB)]
            for k in range(nst):
                sh = 1 << k
                dst, bdst = bufs[k % 2]
                nlo = lo + sh
                P.op("dve", lambda e, dst=dst, src=src, nlo=nlo, sh=sh: e.tensor_tensor(
                    out=dst[:, nlo:TP + 16], in0=src[:, nlo:TP + 16], in1=src[:, nlo - sh:TP + 16 - sh], op=ALU.add),
                    reads=[bsrc], writes=[bdst])
                src, bsrc, lo = dst, bdst, nlo
            assert lo == w
            P.op("dve", lambda e, src=src: e.scalar_tensor_tensor(
                out=R[ci][:], in0=src[:, 16:TP + 16], scalar=1.0 / w, in1=u[:, 16:TP + 16],
                op0=ALU.mult, op1=ALU.subtract),
                reads=[bsrc, bU[ci]], writes=[bR[ci]])
            if half == 0:
                P.op("dve", lambda e, src=src: e.tensor_tensor(out=tfix[:, 0:w - 1], in0=src[:, 16:16 + w - 1],
                                                               in1=invc[:, 0:w - 1], op=ALU.mult),
                     reads=[bsrc, bconst, bcv, bpw, bfg], writes=[btfix])
                P.op("dve", lambda e: e.tensor_tensor(out=R[ci][:, 0:w - 1], in0=tfix[:, 0:w - 1],
                                                      in1=u[:, 16:16 + w - 1], op=ALU.subtract),
                     reads=[btfix, bU[ci]], writes=[bR[ci]])

        def pool_B(p, g):
            for oi in range(2):
                c = 2 * g + oi
                ti = tcnt["n"] % 2
                tcnt["n"] += 1
                zg_ = z_mm(p, 8 + c)
                P.op("act", lambda e, zg_=zg_, c=c, ti=ti: e.activation(out=T1[ti][:], in_=Z[zg_][:], func=AF.Silu,
                                                                       bias=c_bin(8 + c), scale=1.0),
                     reads=[bZ[zg_], bconst, bcv, bpw, bfg], writes=[bT1[ti]])
                zb = next_z()

                def mm(e, zb=zb, oi=oi):
                    ins = None
                    for sblk in range(2):
                        for k in range(2):
                            ins = e.matmul(Z[zb][:, sblk * 512:(sblk + 1) * 512],
                                           pw[:, g, k, oi * 128:(oi + 1) * 128],
                                           R[k][:, sblk * 512:(sblk + 1) * 512], start=(k == 0), stop=(k == 1))
                    return ins
                P.op("pe", mm, reads=[bR[0], bR[1], bconst, bcv, bpw, bfg], writes=[bZ[zb]])
                P.op("dve", lambda e, zb=zb, c=c, ti=ti: e.scalar_tensor_tensor(
                    out=ya[:, c, :], in0=Z[zb][:], scalar=c_psc(c), in1=T1[ti][:], op0=ALU.mult, op1=ALU.mult),
                    reads=[bZ[zb], bT1[ti], bconst, bcv, bpw, bfg], writes=[bya[c]])

        def merge_chunk(p, j):
            ti = tcnt["n"] % 2
            tcnt["n"] += 1
            t1, t2, t3 = T1[ti], T2[ti], T3[ti]
            zoa = o_mm(p, "a", j, ya, bya)
            zma = z_mm(p, 48 + j)
            P.op("act", lambda e: e.activation(out=t1[:], in_=Z[zma][:], func=AF.Sigmoid, bias=c_bin(48 + j), scale=1.0),
                 reads=[bZ[zma], bconst, bcv, bpw, bfg], writes=[bT1[ti]])
            P.op("dve", lambda e: e.tensor_tensor(out=t3[:], in0=Z[zoa][:], in1=t1[:], op=ALU.mult),
                 reads=[bZ[zoa], bT1[ti]], writes=[bT3[ti]])
            zob = o_mm(p, "b", j, yb, byb)
            zmb = z_mm(p, 56 + j)
            P.op("act", lambda e: e.activation(out=t2[:], in_=Z[zmb][:], func=AF.Sigmoid, bias=c_bin(56 + j), scale=1.0),
                 reads=[bZ[zmb], bconst, bcv, bpw, bfg], writes=[bT2[ti]])
            P.op("dve", lambda e: e.tensor_tensor(out=t2[:], in0=Z[zob][:], in1=t2[:], op=ALU.mult),
                 reads=[bZ[zob], bT2[ti]], writes=[bT2[ti]])
            P.op("dve", lambda e: e.tensor_tensor(out=mg[:, j, :], in0=t3[:], in1=t2[:], op=ALU.add),
                 reads=[bT3[ti], bT2[ti]], writes=[bmg[j]])

        def load_wo(b):
            def dma_k(k):
                P.dma("pool", "swo%d" % k,
                      lambda e: e.dma_start(out=wo[:, k, :], in_=wo_d[:, k * 1024:(k + 1) * 1024]),
                      writes=[bwo[k]])

            def scale_k(k):
                P.op("pool", lambda e: e.tensor_tensor(out=wo[:, k, :], in0=wo[:, k, :], in1=gbc[:, b, :], op=ALU.mult),
                     reads=[bwo[k], bgbc], writes=[bwo[k]])
            items = [lambda: dma_k(0)]
            for k in range(8):
                if k + 1 < 8:
                    items.append(lambda k=k: dma_k(k + 1))
                items.append(lambda k=k: scale_k(k))
            pool_work.extend(items)

        def tail_xload(p, t):
            gi = p * 8 + t
            s = gi % NXN
            P.dma("sp", "xl%d" % s,
                  lambda e: e.dma_start(out=XN[s][:], in_=x_d[gi * 128:(gi + 1) * 128, :]),
                  writes=[bXN[s]])

        def tail_mm(p, t):
            gi = p * 8 + t
            s = gi % NXN
            xn = XN[s]
            zb = next_z()

            def mm(e):
                ins = None
                for dh in range(2):
                    for k in range(8):
                        ins = e.matmul(Z[zb][:, dh * 512:(dh + 1) * 512], mg[:, k, t * 128:(t + 1) * 128],
                                       wo[:, k, dh * 512:(dh + 1) * 512], start=(k == 0), stop=(k == 7))
                return ins
            if t == 0:
                def mm_a(e):
                    ins = None
                    for dh in range(2):
                        for k in range(7):
                            ins = e.matmul(Z[zb][:, dh * 512:(dh + 1) * 512], mg[:, k, t * 128:(t + 1) * 128],
                                           wo[:, k, dh * 512:(dh + 1) * 512], start=(k == 0), stop=False)
                    return ins

                def mm_b(e):
                    ins = None
                    for dh in range(2):
                        ins = e.matmul(Z[zb][:, dh * 512:(dh + 1) * 512], mg[:, 7, t * 128:(t + 1) * 128],
                                       wo[:, 7, dh * 512:(dh + 1) * 512], start=False, stop=True)
                    return ins
                P.op("pe", mm_a, reads=bmg[0:7] + bwo, writes=[bZ[zb]])
                P.op("pe", mm_b, reads=bmg + bwo, writes=[bZ[zb]])
            else:
                P.op("pe", mm, reads=bmg + bwo, writes=[bZ[zb]])
            P.op("dve", lambda e: e.tensor_tensor(out=xn[:], in0=Z[zb][:], in1=xn[:], op=ALU.add),
                 reads=[bZ[zb], bXN[s]], writes=[bXN[s]])

        def tail_fa(p, t):
            gi = p * 8 + t
            s = gi % NXN
            q = gi % 2
            xn = XN[s]
            ssc = nwt[:, 16 + 8 * q + 3:16 + 8 * q + 4]
            P.op("act", lambda e: e.activation(out=junk[:], in_=xn[:], func=AF.Square, accum_out=ssc),
                 reads=[bXN[s]], writes=[bjunk, bnwt[q]])
            newton(ssc, 16 + 8 * q, bnwt[q], [bnwt[q]], 3)

        def tail_fb(p, t):
            gi = p * 8 + t
            s = gi % NXN
            q = gi % 2
            xn = XN[s]
            y = nwt[:, 16 + 8 * q + 1:16 + 8 * q + 2]
            P.op("dve", lambda e: e.scalar_tensor_tensor(out=xn[:], in0=xn[:], scalar=y, in1=fgbc[:],
                                                         op0=ALU.mult, op1=ALU.mult),
                 reads=[bXN[s], bnwt[q], bconst, bcv, bpw, bfg], writes=[bXN[s]])
            P.dma("sp", "out%d" % s,
                  lambda e: e.dma_start(out=y_d[gi * 128:(gi + 1) * 128, :], in_=xn[:]),
                  reads=[bXN[s]])

        load_x(0, 0)
        load_x(0, 1)
        stg = [(T1[0], bT1[0]), (T1[1], bT1[1]), (T2[0], bT2[0]), (T2[1], bT2[1]), (T3[0], bT3[0]), (T3[1], bT3[1])]
        for gi in range(2, 8):
            buf, bb = stg[gi - 2]
            stage[gi] = (buf, bb)
            P.dma("sp", "xs%d" % gi,
                  lambda e, buf=buf, gi=gi: e.dma_start(out=buf[:], in_=x_d[gi * 128:(gi + 1) * 128, :]),
                  writes=[bb])
        setup()
        prep_front(0, 0)
        for t in range(8):
            if t + 1 < 8:
                prep_front(0, t + 1)
            prep_norm(0, t)
            prep_back_raw(t)
            if t == 4:
                setup_b1()
        setup_b2()
        def tail_step(pp, i):
            if 0 <= i < 8:
                tail_mm(pp, i)
            if 0 <= i - 1 < 8:
                tail_fa(pp, i - 1)
            if 0 <= i - 2 < 8:
                tail_fb(pp, i - 2)
                if i - 2 + NXN < 8:
                    tail_xload(pp, i - 2 + NXN)

        def seq_of(pp):
            return (pp * TP) // SEQ

        for p in range(NPASS):
            nxt = p + 1 < NPASS
            for i in range(8):
                g = i // 2
                if nxt and i == 6:
                    prep_front(p + 1, 0)
                if nxt and i == 7:
                    prep_norm(p + 1, 0)
                    prep_front(p + 1, 1)
                if i % 2 == 0:
                    pool_A(p, g)
                    conv_chunk(p, i, hook1=lambda: pool_chain(p, g, 0), hook2=lambda: pool_chain(p, g, 1))
                else:
                    pool_B(p, g)
                    conv_chunk(p, i)
                if p == 0 and i == 1:
                    setup_gate()
                if p > 0:
                    tail_step(p - 1, i)
                drain_pool_work(2)
            if p > 0:
                tail_step(p - 1, 8)
                tail_step(p - 1, 9)
                if seq_of(p) != seq_of(p - 1):
                    load_wo(seq_of(p))
            for t in range(NXN):
                tail_xload(p, t)
            if nxt:
                prep_norm(p + 1, 1)
                prep_back(p + 1, 0)
            for j in range(8):
                if nxt and j + 2 < 8:
                    prep_front(p + 1, j + 2)
                merge_chunk(p, j)
                if nxt:
                    if j + 2 < 8:
                        prep_norm(p + 1, j + 2)
                    if j + 1 < 8:
                        prep_back(p + 1, j + 1)
                drain_pool_work(2)
            drain_pool_work(16)
        for i in range(10):
            tail_step(NPASS - 1, i)
        P.final_wait("sp")
        P.emit()
    assert len(order) == NCH
    return nc, order


def _chunk_layout(w, c):
    blk = w[:, c * 128:(c + 1) * 128].reshape(8, 128, 128)
    return np.ascontiguousarray(blk.transpose(1, 0, 2)).reshape(128, 1024)


def _host_layout(inputs, nseq, ncores, order):
    f = lambda a: np.ascontiguousarray(np.asarray(a, dtype=np.float32))
    x = f(inputs["x"]); c = f(inputs["c"])
    ada_w = f(inputs["ada_w"])[0]; ada_b = f(inputs["ada_b"])[0]; norm_g = f(inputs["norm_g"])[0]
    w_in = f(inputs["w_in"])[0]; b_in = f(inputs["b_in"])[0]; pool_w = f(inputs["pool_w"])[0]
    pool_scale = f(inputs["pool_scale"])[0]; conv_w = f(inputs["conv_w"])[0]; conv_b = f(inputs["conv_b"])[0]
    w_out_a = f(inputs["w_out_a"])[0]; w_out_b = f(inputs["w_out_b"])[0]; w_o = f(inputs["w_o"])[0]
    final_g = f(inputs["final_g"])

    ws = np.empty((NCH, 128, 1024), np.float32)
    for n, (kind, idx) in enumerate(order):
        src = {"z": w_in, "a": w_out_a, "b": w_out_b}[kind]
        ws[n] = _chunk_layout(src, idx)
    adaw = np.empty((3, 128, 8192), np.float32)
    for i in range(3):
        piece = ada_w[:, i * 1024:(i + 1) * 1024].reshape(8, 128, 1024)
        adaw[i] = piece.transpose(1, 0, 2).reshape(128, 8192)
    wo = np.ascontiguousarray(w_o.reshape(8, 128, 1024).transpose(1, 0, 2)).reshape(128, 8192)
    pw = np.ascontiguousarray(pool_w.reshape(4, 2, 128, 256).transpose(2, 0, 1, 3)).reshape(128, 2048)
    cvec = np.empty((128, 136), np.float32)
    cvec[:, 0:24] = ada_b.reshape(24, 128).T
    cvec[:, 24:32] = norm_g.reshape(8, 128).T
    cvec[:, 32:96] = b_in.reshape(64, 128).T
    cvec[:, 96:104] = pool_scale.reshape(8, 128).T
    cvec[:, 104:128] = conv_w.reshape(3, 8, 128).transpose(2, 1, 0).reshape(128, 24)
    cvec[:, 128:136] = conv_b.reshape(8, 128).T
    bcv = np.stack([ada_b[2048:3072], final_g]).astype(np.float32)

    in_maps = []
    for i in range(ncores):
        xs = x[i * nseq:(i + 1) * nseq].reshape(nseq * SEQ, D)
        cs = c[i * nseq:(i + 1) * nseq]
        cT = np.ascontiguousarray(cs.reshape(nseq, 8, 128).transpose(2, 1, 0)).reshape(128, 8 * nseq)
        in_maps.append({"x": np.ascontiguousarray(xs), "cT": cT, "ws": ws, "adaw": adaw, "wo": wo, "pw": pw,
                        "cvec": cvec, "bcv": bcv})
    return in_maps


_NC_CACHE = {}


def kernel(**inputs):
    B = np.asarray(inputs["x"]).shape[0]
    nseq = B // NCORES
    if nseq not in _NC_CACHE:
        _NC_CACHE[nseq] = build_nc(nseq)
    nc, order = _NC_CACHE[nseq]
    in_maps = _host_layout(inputs, nseq, NCORES, order)
    res = run_bass_kernel_spmd(nc, in_maps, core_ids=list(range(NCORES)))
    out = np.stack([r["y"].reshape(nseq, SEQ, D) for r in res.results], axis=0)
    return out.reshape(B, SEQ, D).astype(np.float32)
```
